# Optimizing a Trainium2 kernel written in Bass

```python
import math, functools
import jax, jax.numpy as jnp
from jax import lax
import numpy as np

D_MODEL = 1024
BATCH = 4
SEQ = 4096
DEPTH = 1
DEC_BATCH = 128
DEC_SEQ = 8
PAST_LEN = 8192
PAGE_SIZE = 128

SSM_WIDTH = D_MODEL // 2
SSM_GROUP = 16
SSM_GROUPS = SSM_WIDTH // SSM_GROUP
SSM_STATE = 64
DT_MIN = 0.001
DT_MAX = 0.1
N_HEADS = 8
Q_LORA = 384
KV_LORA = 256
QK_NOPE = 64
QK_ROPE = 32
QK_HEAD = QK_NOPE + QK_ROPE
V_HEAD = 64
ROPE_THETA = 10000.0
ATTN_SCALE = QK_HEAD ** -0.5
Q_BLOCK = 128
D_FF = ((8 * D_MODEL // 3 + 255) // 256) * 256
PLE_DIM = 256
EPS = 1e-6
SPLIT_POINTS = (SSM_WIDTH,
                SSM_WIDTH + Q_LORA,
                SSM_WIDTH + Q_LORA + KV_LORA,
                SSM_WIDTH + Q_LORA + KV_LORA + QK_ROPE,
                SSM_WIDTH + Q_LORA + KV_LORA + QK_ROPE + D_MODEL)
D_IN = SPLIT_POINTS[-1] + D_MODEL

kernel_name = 'hybrid_s5_mla_decoder_step'


def rms_norm(x, g):
    xf = x.astype(jnp.float32)
    y = xf * lax.rsqrt(jnp.mean(xf * xf, axis=-1, keepdims=True) + EPS)
    return (y * g.astype(jnp.float32)).astype(x.dtype)


def qk_norm(x_nope, x_rope, g_nope, g_rope_half):
    nf = x_nope.astype(jnp.float32)
    rf = x_rope.astype(jnp.float32)
    ms = (jnp.sum(nf * nf, -1, keepdims=True) + jnp.sum(rf * rf, -1, keepdims=True)) / QK_HEAD
    r = lax.rsqrt(ms + EPS)
    g_rope = jnp.concatenate([g_rope_half, g_rope_half]).astype(jnp.float32)
    return ((nf * r * g_nope.astype(jnp.float32)).astype(x_nope.dtype),
            (rf * r * g_rope).astype(x_rope.dtype))


def rope_tables(pos):
    half = QK_ROPE // 2
    inv = ROPE_THETA ** (-jnp.arange(half, dtype=jnp.float32) / half)
    ang = pos.astype(jnp.float32)[:, None] * inv[None, :]
    return jnp.cos(ang), jnp.sin(ang)


def apply_rope(x, cos, sin):
    half = QK_ROPE // 2
    xf = x.astype(jnp.float32)
    x1, x2 = xf[..., :half], xf[..., half:]
    return jnp.concatenate([x1 * cos - x2 * sin, x2 * cos + x1 * sin], -1).astype(x.dtype)


def ssm_discretize(lam_re, lam_im, log_dt, b_re, b_im):
    lr, li = lam_re.astype(jnp.float32), lam_im.astype(jnp.float32)
    dt = jnp.exp(log_dt.astype(jnp.float32))[:, None]
    mag = jnp.exp(lr * dt)
    lb_re, lb_im = mag * jnp.cos(li * dt), mag * jnp.sin(li * dt)
    num_re, num_im = lb_re - 1.0, lb_im
    den = lr * lr + li * li
    f_re = (num_re * lr + num_im * li) / den
    f_im = (num_im * lr - num_re * li) / den
    br, bi = b_re.astype(jnp.float32), b_im.astype(jnp.float32)
    bb_re = f_re[..., None] * br - f_im[..., None] * bi
    bb_im = f_re[..., None] * bi + f_im[..., None] * br
    return lb_re, lb_im, bb_re, bb_im


def _complex_affine_combine(e1, e2):
    a1r, a1i, b1r, b1i = e1
    a2r, a2i, b2r, b2i = e2
    return (a1r * a2r - a1i * a2i,
            a1r * a2i + a1i * a2r,
            a2r * b1r - a2i * b1i + b2r,
            a2r * b1i + a2i * b1r + b2i)


def ssm_scan(u, h0_re, h0_im, lb_re, lb_im, bb_re, bb_im, c_re, c_im, d):
    bsz, t = u.shape[:2]
    uf = u.astype(jnp.float32)
    ug = uf.reshape(bsz, t, SSM_GROUPS, SSM_GROUP)
    bu_re = jnp.einsum('btgc,gnc->btgn', ug, bb_re)
    bu_im = jnp.einsum('btgc,gnc->btgn', ug, bb_im)
    h0r, h0i = h0_re.astype(jnp.float32), h0_im.astype(jnp.float32)
    bu_re = bu_re.at[:, 0].add(lb_re * h0r - lb_im * h0i)
    bu_im = bu_im.at[:, 0].add(lb_re * h0i + lb_im * h0r)
    a_re = jnp.broadcast_to(lb_re, bu_re.shape)
    a_im = jnp.broadcast_to(lb_im, bu_im.shape)
    _, _, h_re, h_im = lax.associative_scan(_complex_affine_combine, (a_re, a_im, bu_re, bu_im), axis=1)
    y = (jnp.einsum('btgn,gcn->btgc', h_re, c_re.astype(jnp.float32))
         - jnp.einsum('btgn,gcn->btgc', h_im, c_im.astype(jnp.float32)))
    y = y.reshape(bsz, t, SSM_WIDTH) + d.astype(jnp.float32) * uf
    return y.astype(u.dtype), h_re[:, -1], h_im[:, -1]


def mla_keys_values(ckv, kpe, lw):
    k_nope = jnp.einsum('...tc,chd->...thd', ckv, lw['w_uk'])
    v = jnp.einsum('...tc,chd->...thd', ckv, lw['w_uv'])
    k_rope = jnp.broadcast_to(kpe[..., None, :], k_nope.shape[:-1] + (QK_ROPE,))
    k_nope, k_rope = qk_norm(k_nope, k_rope, lw['k_norm_nope_g'], lw['k_norm_rope_g'])
    return jnp.concatenate([k_nope, k_rope], -1), v


def prompt_attend(q, ckv, kpe, lw):
    k, v = mla_keys_values(ckv, kpe, lw)
    bsz, t = q.shape[:2]
    kpos = jnp.arange(t)

    def block(i):
        qb = lax.dynamic_slice_in_dim(q, i * Q_BLOCK, Q_BLOCK, axis=1)
        s = jnp.einsum('bqhd,bkhd->bhqk', qb, k).astype(jnp.float32) * ATTN_SCALE
        qpos = i * Q_BLOCK + jnp.arange(Q_BLOCK)
        s = jnp.where(kpos[None, :] <= qpos[:, None], s, -jnp.inf)
        pr = jax.nn.softmax(s, axis=-1).astype(v.dtype)
        return jnp.einsum('bhqk,bkhd->bqhd', pr, v)

    out = lax.map(block, jnp.arange(t // Q_BLOCK))
    return out.transpose(1, 0, 2, 3, 4).reshape(bsz, t, N_HEADS, V_HEAD)


def sample_attend(q, ckv, kpe, lw, cache_ckv_l, cache_kpe_l, page_table):
    past = page_table.shape[1] * PAGE_SIZE
    tq = q.shape[1]
    kidx = jnp.arange(past + tq)
    qidx = past + jnp.arange(tq)
    mask = kidx[None, :] <= qidx[:, None]

    def one(args):
        q_b, ckv_b, kpe_b, pt = args
        ckv_all = jnp.concatenate([cache_ckv_l[pt].reshape(-1, KV_LORA), ckv_b.astype(cache_ckv_l.dtype)], 0)
        kpe_all = jnp.concatenate([cache_kpe_l[pt].reshape(-1, QK_ROPE), kpe_b.astype(cache_kpe_l.dtype)], 0)
        k, v = mla_keys_values(ckv_all, kpe_all, lw)
        s = jnp.einsum('qhd,khd->hqk', q_b, k).astype(jnp.float32) * ATTN_SCALE
        s = jnp.where(mask[None], s, -jnp.inf)
        pr = jax.nn.softmax(s, axis=-1).astype(v.dtype)
        return jnp.einsum('hqk,khd->qhd', pr, v)

    return lax.map(one, (q, ckv, kpe, page_table))


def decoder_layer(x, p, cos, sin, h0_re, h0_im, attend, lw):
    bsz, t = x.shape[:2]
    h = rms_norm(x, lw['norm_attn_g'])
    proj = h @ lw['w_in']
    u, q_lat, kv_lat, k_rope, g_a, g_b = jnp.split(proj, SPLIT_POINTS, axis=-1)
    lb_re, lb_im, bb_re, bb_im = ssm_discretize(lw['ssm_lam_re'], lw['ssm_lam_im'], lw['ssm_log_dt'],
                                                lw['ssm_b_re'], lw['ssm_b_im'])
    y_ssm, hT_re, hT_im = ssm_scan(u, h0_re, h0_im, lb_re, lb_im, bb_re, bb_im,
                                   lw['ssm_c_re'], lw['ssm_c_im'], lw['ssm_d'])
    z = jax.nn.gelu(y_ssm)
    a_out = z * jax.nn.sigmoid(z @ lw['w_glu'])
    qc = rms_norm(q_lat, lw['q_norm_g'])
    q = jnp.einsum('btc,chd->bthd', qc, lw['w_uq'])
    q_nope = q[..., :QK_NOPE]
    q_rope = apply_rope(q[..., QK_NOPE:], cos[:, None, :], sin[:, None, :])
    q_nope, q_rope = qk_norm(q_nope, q_rope, lw['q_norm_nope_g'], lw['q_norm_rope_g'])
    q = jnp.concatenate([q_nope, q_rope], -1)
    ckv = rms_norm(kv_lat, lw['kv_norm_g'])
    kpe = apply_rope(k_rope, cos, sin)
    b_out = attend(q, ckv, kpe).reshape(bsz, t, N_HEADS * V_HEAD)
    merged = (jax.nn.sigmoid(g_a) * (a_out @ lw['w_branch_a'])
              + jax.nn.sigmoid(g_b) * (b_out @ lw['w_branch_b']))
    x = x + merged @ lw['w_out']
    hf = rms_norm(x, lw['norm_ffn_g'])
    x = x + (jax.nn.silu(hf @ lw['w_gate']) * (hf @ lw['w_up'])) @ lw['w_down']
    gate = jax.nn.sigmoid(rms_norm(x, lw['norm_ple_g']) @ lw['w_ple_gate'])
    x = x + gate * (p @ lw['w_ple'])
    return x, ckv, kpe, hT_re, hT_im


def setup_inputs(seed: int = 0) -> dict:
    key = jax.random.key(seed)
    keys = jax.random.split(key, 64)
    ks = iter([keys[i] for i in range(64)])

    def nrm(shape, scale):
        return jax.random.normal(next(ks), shape, jnp.float32) * scale

    def gain(shape):
        return 1.0 + 0.01 * jax.random.normal(next(ks), shape, jnp.float32)

    n_pages = PAST_LEN // PAGE_SIZE
    n_used = DEC_BATCH * n_pages
    n_pool = n_used + n_used // 4
    L = DEPTH
    inp = {}
    inp['x_prompt'] = nrm((BATCH, SEQ, D_MODEL), 1.0)
    inp['x_sample'] = nrm((DEC_BATCH, DEC_SEQ, D_MODEL), 1.0)
    inp['p_prompt'] = nrm((L, BATCH, SEQ, PLE_DIM), 1.0)
    inp['p_sample'] = nrm((L, DEC_BATCH, DEC_SEQ, PLE_DIM), 1.0)
    inp['cache_ckv'] = nrm((L, n_pool, PAGE_SIZE, KV_LORA), 1.0)
    inp['cache_kpe'] = nrm((L, n_pool, PAGE_SIZE, QK_ROPE), 1.0)
    inp['state_ssm_re'] = nrm((L, DEC_BATCH, SSM_GROUPS, SSM_STATE), 0.3)
    inp['state_ssm_im'] = nrm((L, DEC_BATCH, SSM_GROUPS, SSM_STATE), 0.3)
    inp['page_table'] = jax.random.permutation(next(ks), n_pool)[:n_used].reshape(DEC_BATCH, n_pages).astype(jnp.int32)
    inp['norm_attn_g'] = gain((L, D_MODEL))
    inp['w_in'] = nrm((L, D_MODEL, D_IN), D_MODEL ** -0.5)
    inp['q_norm_g'] = gain((L, Q_LORA))
    inp['w_uq'] = nrm((L, Q_LORA, N_HEADS, QK_HEAD), Q_LORA ** -0.5)
    inp['kv_norm_g'] = gain((L, KV_LORA))
    inp['w_uk'] = nrm((L, KV_LORA, N_HEADS, QK_NOPE), KV_LORA ** -0.5)
    inp['w_uv'] = nrm((L, KV_LORA, N_HEADS, V_HEAD), KV_LORA ** -0.5)
    inp['q_norm_nope_g'] = gain((L, QK_NOPE))
    inp['q_norm_rope_g'] = gain((L, QK_ROPE // 2))
    inp['k_norm_nope_g'] = gain((L, QK_NOPE))
    inp['k_norm_rope_g'] = gain((L, QK_ROPE // 2))
    inp['ssm_lam_re'] = -0.5 + nrm((L, SSM_GROUPS, SSM_STATE), 0.01)
    inp['ssm_lam_im'] = (math.pi * jnp.arange(SSM_STATE, dtype=jnp.float32))[None, None, :] + nrm((L, SSM_GROUPS, SSM_STATE), 0.01)
    inp['ssm_log_dt'] = jax.random.uniform(next(ks), (L, SSM_GROUPS), jnp.float32, math.log(DT_MIN), math.log(DT_MAX))
    inp['ssm_b_re'] = nrm((L, SSM_GROUPS, SSM_STATE, SSM_GROUP), (2 * SSM_GROUP) ** -0.5)
    inp['ssm_b_im'] = nrm((L, SSM_GROUPS, SSM_STATE, SSM_GROUP), (2 * SSM_GROUP) ** -0.5)
    inp['ssm_c_re'] = nrm((L, SSM_GROUPS, SSM_GROUP, SSM_STATE), SSM_STATE ** -0.5)
    inp['ssm_c_im'] = nrm((L, SSM_GROUPS, SSM_GROUP, SSM_STATE), SSM_STATE ** -0.5)
    inp['ssm_d'] = nrm((L, SSM_WIDTH), 1.0)
    inp['w_glu'] = nrm((L, SSM_WIDTH, SSM_WIDTH), SSM_WIDTH ** -0.5)
    inp['w_branch_a'] = nrm((L, SSM_WIDTH, D_MODEL), SSM_WIDTH ** -0.5)
    inp['w_branch_b'] = nrm((L, N_HEADS * V_HEAD, D_MODEL), (N_HEADS * V_HEAD) ** -0.5)
    inp['w_out'] = nrm((L, D_MODEL, D_MODEL), D_MODEL ** -0.5)
    inp['norm_ffn_g'] = gain((L, D_MODEL))
    inp['w_gate'] = nrm((L, D_MODEL, D_FF), D_MODEL ** -0.5)
    inp['w_up'] = nrm((L, D_MODEL, D_FF), D_MODEL ** -0.5)
    inp['w_down'] = nrm((L, D_FF, D_MODEL), D_FF ** -0.5)
    inp['norm_ple_g'] = gain((L, D_MODEL))
    inp['w_ple_gate'] = nrm((L, D_MODEL, D_MODEL), D_MODEL ** -0.5)
    inp['w_ple'] = nrm((L, PLE_DIM, D_MODEL), PLE_DIM ** -0.5)
    return inp


def reference(x_prompt, x_sample, p_prompt, p_sample, cache_ckv, cache_kpe, state_ssm_re, state_ssm_im,
              page_table, norm_attn_g, w_in, q_norm_g, w_uq, kv_norm_g, w_uk, w_uv, q_norm_nope_g,
              q_norm_rope_g, k_norm_nope_g, k_norm_rope_g, ssm_lam_re, ssm_lam_im, ssm_log_dt, ssm_b_re,
              ssm_b_im, ssm_c_re, ssm_c_im, ssm_d, w_glu, w_branch_a, w_branch_b, w_out, norm_ffn_g,
              w_gate, w_up, w_down, norm_ple_g, w_ple_gate, w_ple):
    cos_p, sin_p = rope_tables(jnp.arange(x_prompt.shape[1]))
    cos_s, sin_s = rope_tables(PAST_LEN + jnp.arange(x_sample.shape[1]))
    h0_prompt = jnp.zeros((x_prompt.shape[0], SSM_GROUPS, SSM_STATE), jnp.float32)
    xp, xs = x_prompt, x_sample
    ckv_p, kpe_p, hre_p, him_p = [], [], [], []
    ckv_s, kpe_s, hre_s, him_s = [], [], [], []
    for i in range(DEPTH):
        lw = dict(norm_attn_g=norm_attn_g[i], w_in=w_in[i], q_norm_g=q_norm_g[i], w_uq=w_uq[i],
                  kv_norm_g=kv_norm_g[i], w_uk=w_uk[i], w_uv=w_uv[i], q_norm_nope_g=q_norm_nope_g[i],
                  q_norm_rope_g=q_norm_rope_g[i], k_norm_nope_g=k_norm_nope_g[i],
                  k_norm_rope_g=k_norm_rope_g[i], ssm_lam_re=ssm_lam_re[i], ssm_lam_im=ssm_lam_im[i],
                  ssm_log_dt=ssm_log_dt[i], ssm_b_re=ssm_b_re[i], ssm_b_im=ssm_b_im[i],
                  ssm_c_re=ssm_c_re[i], ssm_c_im=ssm_c_im[i], ssm_d=ssm_d[i], w_glu=w_glu[i],
                  w_branch_a=w_branch_a[i], w_branch_b=w_branch_b[i], w_out=w_out[i],
                  norm_ffn_g=norm_ffn_g[i], w_gate=w_gate[i], w_up=w_up[i], w_down=w_down[i],
                  norm_ple_g=norm_ple_g[i], w_ple_gate=w_ple_gate[i], w_ple=w_ple[i])
        attend_p = functools.partial(prompt_attend, lw=lw)
        attend_s = functools.partial(sample_attend, lw=lw, cache_ckv_l=cache_ckv[i],
                                     cache_kpe_l=cache_kpe[i], page_table=page_table)
        xp, c1, k1, r1, m1 = decoder_layer(xp, p_prompt[i], cos_p, sin_p, h0_prompt, h0_prompt, attend_p, lw)
        xs, c2, k2, r2, m2 = decoder_layer(xs, p_sample[i], cos_s, sin_s, state_ssm_re[i], state_ssm_im[i], attend_s, lw)
        ckv_p.append(c1); kpe_p.append(k1); hre_p.append(r1); him_p.append(m1)
        ckv_s.append(c2); kpe_s.append(k2); hre_s.append(r2); him_s.append(m2)
    return (xp, xs,
            jnp.stack(ckv_p), jnp.stack(kpe_p), jnp.stack(hre_p), jnp.stack(him_p),
            jnp.stack(ckv_s), jnp.stack(kpe_s), jnp.stack(hre_s), jnp.stack(him_s))
```

```python
import math
from contextlib import ExitStack
import numpy as np
import concourse.bass as bass
import concourse.mybir as mybir
from concourse.bass_utils import run_bass_kernel_spmd

F32 = mybir.dt.float32
BF16 = mybir.dt.bfloat16
I32 = mybir.dt.int32
U32 = mybir.dt.uint32
AF = mybir.ActivationFunctionType
ALU = mybir.AluOpType
AX = mybir.AxisListType

D = 1024
D_IN = 3232
D_FF = 2816
PLE = 256
EPS = 1e-6
ATTN_SCALE = 96 ** -0.5
NT = 256
C_U, C_Q, C_KV, C_KR, C_GA, C_GB = 0, 512, 896, 1152, 1184, 2208


class R:
    __slots__ = ("w", "rd")

    def __init__(self):
        self.w = None
        self.rd = {}


class P:
    ENG = ("pe", "act", "dve", "pool", "sp")

    def __init__(self, nc, es):
        self.nc = nc
        self.es = es
        self.q = {e: [] for e in self.ENG}
        self.sem = {}
        self.cnt = {}
        self.epoch = {}
        self.EPOCH_MAX = 30000
        for e in self.ENG:
            self.epoch[e] = 0
            k = e + "#0"
            self.sem[k] = es.enter_context(nc.semaphore("s_" + e + "_0"))
            self.cnt[k] = 0
        self.dq = {}
        self.NDQ = 8
        for qn in ("sp", "pool"):
            for i in range(self.NDQ):
                k = f"d_{qn}{i}"
                self.sem[k] = es.enter_context(nc.semaphore(k))
                self.cnt[k] = 0
            self.dq[qn] = 0
        self.waited = {}
        self.nins = 0

    def _need(self, e, deps):
        for k, v in deps.items():
            if self.waited.get((e, k), 0) < v:
                self.waited[(e, k)] = v
                sem = self.sem[k]
                self.q[e].append(lambda eng, sem=sem, v=v: eng.wait_ge(sem, v))

    def _deps(self, e, reads, writes):
        d = {}

        def add(kv, raw):
            if kv is None:
                return
            k, v = kv
            if k.split("#")[0] == e and (e == "pe" or not raw):
                return
            if d.get(k, 0) < v:
                d[k] = v

        for r in reads:
            add(r.w, True)
        for r in writes:
            add(r.w, False)
            for k, v in r.rd.items():
                add((k, v), False)
        return d

    def op(self, e, fn, reads=(), writes=()):
        self._need(e, self._deps(e, reads, writes))
        ek = e + "#" + str(self.epoch[e])
        if self.cnt[ek] >= self.EPOCH_MAX:
            self.epoch[e] += 1
            ek = e + "#" + str(self.epoch[e])
            self.sem[ek] = self.es.enter_context(self.nc.semaphore("s_" + e + "_" + str(self.epoch[e])))
            self.cnt[ek] = 0
        self.cnt[ek] += 1
        c = self.cnt[ek]
        sem = self.sem[ek]
        self.q[e].append(lambda eng: fn(eng).then_inc(sem, 1))
        self.nins += 1
        for r in reads:
            if r.rd.get(ek, 0) < c:
                r.rd[ek] = c
        for r in writes:
            r.w = (ek, c)
            r.rd = {}

    def dma(self, qn, fn, reads=(), writes=()):
        i = self.dq[qn]
        self.dq[qn] = (i + 1) % self.NDQ
        k = f"d_{qn}{i}"
        d = self._deps(qn, reads, writes)
        if self.cnt[k] > 0:
            d[k] = max(d.get(k, 0), self.cnt[k])
        self._need(qn, d)
        self.cnt[k] += 16
        c = self.cnt[k]
        sem = self.sem[k]
        self.q[qn].append(lambda eng: fn(eng).then_inc(sem, 16))
        self.nins += 1
        for r in reads:
            r.rd[k] = max(r.rd.get(k, 0), c)
        for r in writes:
            r.w = (k, c)
            r.rd = {}

    def barrier(self):
        d = {k: v for k, v in self.cnt.items() if v > 0}
        for e in self.ENG:
            self._need(e, {k: v for k, v in d.items() if k.split("#")[0] != e})

    def finish(self):
        for qn in ("sp", "pool"):
            d = {}
            for i in range(self.NDQ):
                k = f"d_{qn}{i}"
                if self.cnt[k] > 0:
                    d[k] = self.cnt[k]
            self._need(qn, d)
        print("[kernel] instruction counts:", {k: v for k, v in self.cnt.items() if not k.startswith("d_")}, flush=True)
        block = self.es.enter_context(self.nc.Block())
        q = self.q

        @block.tensor
        def _(eng):
            for f in q["pe"]:
                f(eng)

        @block.scalar
        def _(eng):
            for f in q["act"]:
                f(eng)

        @block.vector
        def _(eng):
            for f in q["dve"]:
                f(eng)

        @block.gpsimd
        def _(eng):
            for f in q["pool"]:
                f(eng)

        @block.sync
        def _(eng):
            for f in q["sp"]:
                f(eng)


class Arena:
    def __init__(self, t, nbytes):
        self.t = t
        self.n = nbytes
        self.off = 0

    def reset(self):
        self.off = 0

    def get(self, shape, dt):
        esz = 2 if dt == BF16 else 4
        nel = 1
        for d in shape[1:]:
            nel *= d
        off = (self.off + 3) // 4 * 4
        self.off = off + nel * esz
        assert self.off <= self.n, ("arena overflow", self.off, self.n)
        ap = self.t[0:shape[0], off // 2:(off + nel * esz) // 2]
        if esz == 4:
            ap = ap.bitcast(dt)
        if len(shape) == 3:
            ap = ap.rearrange("q (a b) -> q a b", b=shape[2])
        elif len(shape) == 4:
            ap = ap.rearrange("q (a b c) -> q a b c", b=shape[2], c=shape[3])
        return ap


class Ring:
    def __init__(self, p, name, n, shape, dt, space="sb", arena=None):
        self.t = []
        self.i = 0
        for j in range(n):
            if arena is not None:
                self.t.append((arena.get(shape, dt), R()))
                continue
            if space == "sb":
                t = p.es.enter_context(p.nc.sbuf_tensor(f"{name}{j}", shape, dt))
            else:
                t = p.es.enter_context(p.nc.psum_tensor(f"{name}{j}", shape, dt))
            self.t.append((t, R()))

    def get(self):
        t = self.t[self.i]
        self.i = (self.i + 1) % len(self.t)
        return t


def build(cfg):
    SEQ, NS, NPOOL, NPG = cfg["SEQ"], cfg["NS"], cfg["NPOOL"], cfg["NPG"]
    NPRE = cfg.get("NPRE", 0)
    NPT = SEQ // NT
    KSEQ = (NPRE + NPT) * NT
    NST = NS // 16
    TS = NS * 8
    nc = bass.Bass("TRN2", target_bir_lowering=False)
    es = ExitStack()
    p = P(nc, es)

    def din(name, shape, dt=F32):
        return nc.dram_tensor(name, list(shape), dt, kind="ExternalInput").ap()

    def dout(name, shape, dt=F32):
        return nc.dram_tensor(name, list(shape), dt, kind="ExternalOutput").ap()

    def dscr(name, shape, dt=BF16):
        return nc.dram_tensor(name, list(shape), dt, kind="Internal").ap()

    def sb(name, shape, dt=F32):
        return es.enter_context(nc.sbuf_tensor("t_" + name, list(shape), dt))

    x_p = din("x_p", [SEQ, D]); p_p = din("p_p", [SEQ, PLE])
    if NPRE:
        x_pre = din("x_pre", [NPRE * NT, D])
        ropec_pre = din("ropec_pre", [32, NPRE * NT]); ropes_pre = din("ropes_pre", [32, NPRE * NT])
    pmask_d = din("pmask", [128, 1])
    x_s = din("x_s", [TS, D]); p_s = din("p_s", [TS, PLE])
    cache_ckv = din("cache_ckv", [NPOOL * 32, 1024]); cache_kpe = din("cache_kpe", [NPOOL * 32, 128])
    st_re = din("st_re", [NS, 2048]); st_im = din("st_im", [NS, 2048])
    ptab = din("ptab", [NS, NPG], I32)
    W = {}
    for nm, shp in (("w_in", [D, D_IN]), ("w_uq", [384, 768]), ("w_uk", [256, 512]), ("w_uv", [256, 512]),
                    ("w_ukT", [512, 256]), ("w_glu", [512, 512]), ("w_branch_a", [512, D]),
                    ("w_branch_b", [512, D]), ("w_out", [D, D]), ("w_gate", [D, D_FF]), ("w_up", [D, D_FF]),
                    ("w_down", [D_FF, D]), ("w_ple_gate", [D, D]), ("w_ple", [PLE, D])):
        W[nm] = (din(nm, shp), dscr(nm + "_b", shp), R())
    g_attn = din("g_attn", [D]); g_ffn = din("g_ffn", [D]); g_ple = din("g_ple", [D])
    g_q = din("g_q", [384]); g_kv = din("g_kv", [256])
    g_qn = din("g_qn", [64]); g_qr = din("g_qr", [16]); g_kn = din("g_kn", [64]); g_kr = din("g_kr", [16])
    lam_re = din("lam_re", [32, 64]); lam_im = din("lam_im", [32, 64]); log_dt = din("log_dt", [32])
    b_reT = din("b_reT", [32, 16, 64]); b_imT = din("b_imT", [32, 16, 64])
    c_reT = din("c_reT", [32, 64, 16]); c_imT = din("c_imT", [32, 64, 16])
    ssm_d = din("ssm_d", [512])
    ident_d = din("ident", [128, 128])
    ropec_p = din("ropec_p", [32, SEQ]); ropes_p = din("ropes_p", [32, SEQ])
    ropec_s = din("ropec_s", [32, 128]); ropes_s = din("ropes_s", [32, 128])
    tri_d = din("tri", [128, 128])
    masks_d = din("masks", [128, 16 * 64])

    y_p = dout("y_p", [SEQ, D]); y_s = dout("y_s", [TS, D])
    ckv_p = dout("ckv_p", [SEQ, 256]); kpe_p = dout("kpe_p", [SEQ, 32])
    sre_p = dout("sre_p", [1, 2048]); sim_p = dout("sim_p", [1, 2048])
    ckv_s = dout("ckv_s", [TS, 256]); kpe_s = dout("kpe_s", [TS, 32])
    sre_s = dout("sre_s", [NS, 2048]); sim_s = dout("sim_s", [NS, 2048])

    Kscr = dscr("Kscr", [8, 96, KSEQ]); Kscr_r = R()
    Vscr = dscr("Vscr", [KSEQ // 128, 128, 8 * 65]); Vscr_r = R()

    PS = Ring(p, "ps", 5, [128, 512], F32, "ps")
    spb_t = es.enter_context(nc.psum_tensor("spb", [128, 512], F32))
    SPB = Ring(p, "spb", 0, [128, 64], F32)
    SPB.t = [(spb_t[:, 64 * k:64 * k + 64], R()) for k in range(8)]
    SPBH = R()
    PSO = Ring(p, "pso", 2, [128, 512], F32, "ps")
    WR = Ring(p, "wr", 2, [128, 3072], BF16)

    def act(out, in_, func, reads, writes, scale=None, bias=None, accum=None):
        kw = {}
        if scale is not None:
            kw["scale"] = scale
        if bias is not None:
            kw["bias"] = bias
        if accum is not None:
            kw["accum_out"] = accum
        p.op("act", lambda e: e.activation(out=out, in_=in_, func=func, **kw), reads, writes)

    def tt(eng, out, in0, in1, op, reads, writes):
        p.op(eng, lambda e: e.tensor_tensor(out=out, in0=in0, in1=in1, op=op), reads, writes)

    def ts(eng, out, in0, s1, s2, op0, op1, reads, writes):
        if op1 is None:
            p.op(eng, lambda e: e.tensor_scalar(out=out, in0=in0, scalar1=s1, scalar2=None, op0=op0), reads, writes)
        else:
            p.op(eng, lambda e: e.tensor_scalar(out=out, in0=in0, scalar1=s1, scalar2=s2, op0=op0, op1=op1),
                 reads, writes)

    def stt(out, in0, scalar, in1, op0, op1, reads, writes):
        p.op("dve", lambda e: e.scalar_tensor_tensor(out=out, in0=in0, scalar=scalar, in1=in1, op0=op0, op1=op1),
             reads, writes)

    def cp(eng, out, in_, reads, writes):
        if eng == "act":
            p.op("act", lambda e: e.activation(out=out, in_=in_, func=AF.Copy), reads, writes)
        else:
            p.op(eng, lambda e: e.tensor_copy(out=out, in_=in_), reads, writes)

    def mm(out, lhsT, rhs, start, stop, reads, writes):
        p.op("pe", lambda e: e.matmul(out, lhsT, rhs, start=start, stop=stop), reads, writes)

    def tr(out, in_, ident, reads, writes):
        p.op("pe", lambda e: e.transpose(out, in_, ident), reads, writes)

    def ld(out, in_, reads, writes, q="sp"):
        p.dma(q, lambda e: e.dma_start(out=out, in_=in_, allow_slow_non_contiguous=True), reads, writes)

    def memset(eng, ap, val, writes):
        p.op(eng, lambda e: e.memset(ap, val), (), writes)

    ARENA_BYTES = 42 * 1024
    arena_t = sb("arena", [128, ARENA_BYTES // 2], BF16)
    arena = Arena(arena_t, ARENA_BYTES)

    ident = sb("ident", [128, 128]); ident_r = R()
    ld(ident[:], ident_d, (), (ident_r,))
    identb = sb("identb", [128, 128], BF16); identb_r = R()
    cp("dve", identb[:], ident[:], (ident_r,), (identb_r,))
    ones = sb("ones", [128, 128], BF16); ones_r = R()
    memset("dve", ones[:], 1.0, (ones_r,))
    epsT = sb("epsT", [128, 1]); eps_r = R()
    memset("dve", epsT[:], EPS, (eps_r,))
    pmask = sb("pmask", [128, 1]); pmask_r = R()
    ld(pmask[:], pmask_d, (), (pmask_r,))
    tri = sb("tri", [128, 128]); tri_r = R()
    ld(tri[:], tri_d, (), (tri_r,))
    masks = sb("masks", [128, 16 * 64], BF16); masks_r = R()
    ld(masks[:], masks_d, (), (masks_r,), q="pool")

    def gain_tile(name, src, n, q=128):
        kc = (n + q - 1) // q
        t = sb(name, [128, kc]); r = R()
        with nc.allow_non_contiguous_dma(reason="tiny gain load"):
            full = n // q
            ld(t[0:q, 0:full], src[0:full * q].rearrange("(c q) -> q c", q=q), (), (r,))
            if n > full * q:
                ld(t[0:n - full * q, full:full + 1], src[full * q:n].rearrange("(q c) -> q c", c=1), (r,), (r,))
        return t, r

    gA, gA_r = gain_tile("gA", g_attn, D)
    gF, gF_r = gain_tile("gF", g_ffn, D)
    gP, gP_r = gain_tile("gP", g_ple, D)
    gQ, gQ_r = gain_tile("gQ", g_q, 384)
    gKV, gKV_r = gain_tile("gKV", g_kv, 256)
    dS, dS_r = gain_tile("dS", ssm_d, 512, q=96)
    gqh = sb("gqh", [96, 1]); gqh_r = R()
    gkh = sb("gkh", [96, 1]); gkh_r = R()
    gqk = sb("gqk", [96, 1]); gqk_r = R()
    with nc.allow_non_contiguous_dma(reason="tiny gain load"):
        for t, r, gn, gr in ((gqh, gqh_r, g_qn, g_qr), (gkh, gkh_r, g_kn, g_kr)):
            ld(t[0:64, :], gn.rearrange("(q c) -> q c", c=1), (), (r,))
            ld(t[64:80, :], gr.rearrange("(q c) -> q c", c=1), (r,), (r,))
            ld(t[80:96, :], gr.rearrange("(q c) -> q c", c=1), (r,), (r,))

    for nm, (src, dst, r) in W.items():
        rows = src.shape[0]
        step = 512
        for r0 in range(0, rows, step):
            r1 = min(rows, r0 + step)
            p.dma("pool", lambda e, dst=dst, src=src, r0=r0, r1=r1: e.dma_start(
                out=dst[r0:r1, :], in_=src[r0:r1, :], max_dma_last_dim=4096), (), (r,))

    xtok = sb("xtok", [128, 2, D]); xtok_r = [R(), R()]
    xT = sb("xT", [128, 8, NT]); xT_r = [R() for _ in range(8)]
    xnT = sb("xnT", [128, 8, NT], BF16); xnT_r = [R() for _ in range(8)]
    SQ = Ring(p, "sq", 2, [128, NT], BF16)
    rstd = sb("rstd", [128, NT]); rstd_r = R()
    lnt = sb("lnt", [128, NT]); lnt_r = R()

    def norm_fm(src_aps, src_rs, rows, dim, gain, gain_r, out_aps, out_rs, N, out2=None):
        kc = len(src_aps)
        pt, pr = PS.get()
        for c in range(kc):
            sq, sqr = SQ.get()
            act(sq[0:rows, 0:N], src_aps[c], AF.Square, (src_rs[c],), (sqr,))
            mm(pt[0:rows, 0:N], ones[0:rows, 0:rows], sq[0:rows, 0:N], c == 0, c == kc - 1,
               (sqr, ones_r), (pr,))
        act(lnt[0:rows, 0:N], pt[0:rows, 0:N], AF.Ln, (pr, eps_r), (lnt_r,), scale=1.0 / dim, bias=epsT[0:rows, :])
        act(rstd[0:rows, 0:N], lnt[0:rows, 0:N], AF.Exp, (lnt_r,), (rstd_r,), scale=-0.5)
        for c in range(kc):
            ws = (out_rs[c],) if out2 is None else (out_rs[c],)
            stt(out_aps[c], src_aps[c], gain[0:rows, c:c + 1], rstd[0:rows, 0:N], ALU.mult, ALU.mult,
                (src_rs[c], gain_r, rstd_r), ws)
            if out2 is not None:
                cp("pool", out2[0][c], out_aps[c], (out_rs[c],), (out2[1][c],))

    def linear_fm(wname, col0, ncols, rhs_aps, rhs_rs, N, epi, krows=128, mchunk=128):
        src, wb, wr_ = W[wname]
        kc = len(rhs_aps)
        g = max(mchunk, min(1024, 3072 // kc) // mchunk * mchunk)
        c = 0
        while c < ncols:
            gc = min(g, ncols - c)
            wt, wtr = WR.get()
            wv = wt[:, 0:kc * gc].rearrange("q (k n) -> q k n", k=kc)
            if kc * krows == src.shape[0] and krows == 128:
                ld(wv, wb[:, col0 + c:col0 + c + gc].rearrange("(k q) n -> q k n", q=128), (wr_,), (wtr,))
            else:
                for k in range(kc):
                    rws = min(krows, src.shape[0] - k * krows)
                    ld(wv[0:rws, k, :], wb[k * krows:k * krows + rws, col0 + c:col0 + c + gc], (wr_,), (wtr,))
            for m0 in range(0, gc, mchunk):
                mr = min(mchunk, gc - m0)
                pt, pr = PS.get()
                for k in range(kc):
                    rws = rhs_aps[k].shape[0]
                    mm(pt[0:mr, 0:N], wv[0:rws, k, m0:m0 + mr], rhs_aps[k], k == 0, k == kc - 1,
                       (wtr, rhs_rs[k]), (pr,))
                epi((c + m0) // mchunk, pt, pr, mr)
            c += gc

    uTb = sb("uTb", [96, 6, NT], BF16); uTb_r = [R() for _ in range(6)]
    qlT = sb("qlT", [128, 3, NT]); qlT_r = [R() for _ in range(3)]
    qcT = sb("qcT", [128, 3, NT], BF16); qcT_r = [R() for _ in range(3)]
    kvT = sb("kvT", [128, 2, NT]); kvT_r = [R() for _ in range(2)]
    ckvT = sb("ckvT", [128, 2, NT]); ckvT_r = [R() for _ in range(2)]
    ckvTb = sb("ckvTb", [128, 2, NT], BF16); ckvTb_r = [R() for _ in range(2)]
    sga = sb("sga", [128, 8, NT], BF16); sga_r = [R() for _ in range(8)]
    sgb = sb("sgb", [128, 8, NT], BF16); sgb_r = [R() for _ in range(8)]
    kr1 = sb("kr1", [128, NT]); kr1_r = R()
    kr2 = sb("kr2", [128, NT]); kr2_r = R()
    kpeT = sb("kpeT", [128, NT]); kpeT_r = R()
    rc = sb("rc", [128, NT]); rc_r = R()
    rs_ = sb("rs", [128, NT]); rs_r = R()
    zT = sb("zT", [96, 6, NT]); zT_r = [R() for _ in range(6)]
    zTb = sb("zTb", [96, 6, NT], BF16); zTb_r = [R() for _ in range(6)]
    aoT = sb("aoT", [96, 6, NT], BF16); aoT_r = [R() for _ in range(6)]
    boT = sb("boT", [128, 4, NT], BF16); boT_r = [R() for _ in range(4)]
    mgTb = sb("mgTb", [128, 8, NT], BF16); mgTb_r = [R() for _ in range(8)]
    hT = sb("hT", [128, 22, NT], BF16); hT_r = [R() for _ in range(22)]
    pT = sb("pT", [128, 2, NT], BF16); pT_r = [R() for _ in range(2)]
    ptok = sb("ptok", [128, 2, PLE]); ptok_r = [R(), R()]
    gtT = sb("gtT", [128, 8, NT], BF16); gtT_r = [R() for _ in range(8)]
    tmpA = Ring(p, "tmpA", 2, [128, NT], F32)
    QH = Ring(p, "qh", 1, [96, NT], F32)
    qhb = sb("qhb", [96, 8, NT], BF16); qhb_r = [R() for _ in range(8)]
    qs1, qs1_r, qs2, qs2_r = kr1, kr1_r, kr2, kr2_r
    kh = sb("kh", [96, NT]); kh_r = R()
    khb = sb("khb", [96, NT], BF16); khb_r = R()
    otok = xtok[:, :, 0:256]; otok_r = xtok_r
    ktok = sb("ktok", [128, 2, 32]); ktok_r = [R(), R()]
    vtok = sb("vtok", [128, 2, 8, 65], BF16); vtok_r = [R(), R()]
    memset("pool", vtok[:], 1.0, (vtok_r[0], vtok_r[1]))

    NU = 16
    lr = sb("lr", [128, NU]); li = sb("li", [128, NU]); dtt = sb("dtt", [128, NU])
    ssm_r = R()
    with nc.allow_non_contiguous_dma(reason="tiny ssm param loads"):
        ld(lr[:], lam_re.rearrange("(u g) n -> g n u", g=2).rearrange("g n u -> (g n) u"), (), (ssm_r,))
        ld(li[:], lam_im.rearrange("(u g) n -> g n u", g=2).rearrange("g n u -> (g n) u"), (ssm_r,), (ssm_r,))
        ldt = log_dt.rearrange("(u g) -> g u", g=2)
        ld(dtt[0:64, :], ldt[0:1, :].partition_broadcast(64) if False else ldt[0:1, :].broadcast_to([64, NU]),
           (ssm_r,), (ssm_r,))
        ld(dtt[64:128, :], ldt[1:2, :].broadcast_to([64, NU]), (ssm_r,), (ssm_r,))
    sm = {}
    for nm in ("dt", "rho", "ang", "a1", "a2", "cs", "sn", "fre", "fim", "den", "t1", "t2", "nr", "ni", "fir", "fii"):
        sm[nm] = sb("sm_" + nm, [128, NU])
    smi = sb("smi", [128, NU], I32)
    S = (ssm_r,)

    def sop(eng, f):
        p.op(eng, f, S, S)

    act(sm["dt"][:], dtt[:], AF.Exp, S, S)
    tt("dve", sm["a1"][:], lr[:], sm["dt"][:], ALU.mult, S, S)
    act(sm["rho"][:], sm["a1"][:], AF.Exp, S, S)
    tt("dve", sm["ang"][:], li[:], sm["dt"][:], ALU.mult, S, S)

    def sin_of(out, ang_ap, shift):
        ts("dve", sm["t1"][:], ang_ap, shift, 1.0 / (2 * math.pi), ALU.add, ALU.mult, S, S)
        cp("dve", smi[:], sm["t1"][:], S, S)
        cp("dve", sm["t2"][:], smi[:], S, S)
        tt("dve", sm["t1"][:], sm["t1"][:], sm["t2"][:], ALU.subtract, S, S)
        ts("dve", sm["t2"][:], sm["t1"][:], 0.5, None, ALU.is_gt, None, S, S)
        tt("dve", sm["t1"][:], sm["t1"][:], sm["t2"][:], ALU.subtract, S, S)
        ts("dve", sm["t2"][:], sm["t1"][:], -0.5, None, ALU.is_lt, None, S, S)
        tt("dve", sm["t1"][:], sm["t1"][:], sm["t2"][:], ALU.add, S, S)
        act(out, sm["t1"][:], AF.Sin, S, S, scale=2 * math.pi)

    sin_of(sm["sn"][:], sm["ang"][:], 0.0)
    sin_of(sm["cs"][:], sm["ang"][:], math.pi / 2)
    tt("dve", sm["nr"][:], sm["rho"][:], sm["cs"][:], ALU.mult, S, S)
    ts("dve", sm["nr"][:], sm["nr"][:], -1.0, None, ALU.add, None, S, S)
    tt("dve", sm["ni"][:], sm["rho"][:], sm["sn"][:], ALU.mult, S, S)
    tt("dve", sm["t1"][:], lr[:], lr[:], ALU.mult, S, S)
    tt("dve", sm["t2"][:], li[:], li[:], ALU.mult, S, S)
    tt("dve", sm["den"][:], sm["t1"][:], sm["t2"][:], ALU.add, S, S)
    p.op("dve", lambda e: e.reciprocal(out=sm["den"][:], in_=sm["den"][:]), S, S)
    tt("dve", sm["t1"][:], sm["nr"][:], lr[:], ALU.mult, S, S)
    tt("dve", sm["t2"][:], sm["ni"][:], li[:], ALU.mult, S, S)
    tt("dve", sm["t1"][:], sm["t1"][:], sm["t2"][:], ALU.add, S, S)
    tt("dve", sm["fre"][:], sm["t1"][:], sm["den"][:], ALU.mult, S, S)
    tt("dve", sm["t1"][:], sm["ni"][:], lr[:], ALU.mult, S, S)
    tt("dve", sm["t2"][:], sm["nr"][:], li[:], ALU.mult, S, S)
    tt("dve", sm["t1"][:], sm["t1"][:], sm["t2"][:], ALU.subtract, S, S)
    tt("dve", sm["fim"][:], sm["t1"][:], sm["den"][:], ALU.mult, S, S)
    tt("dve", sm["t1"][:], sm["fre"][:], sm["fre"][:], ALU.mult, S, S)
    tt("dve", sm["t2"][:], sm["fim"][:], sm["fim"][:], ALU.mult, S, S)
    tt("dve", sm["t1"][:], sm["t1"][:], sm["t2"][:], ALU.add, S, S)
    p.op("dve", lambda e: e.reciprocal(out=sm["t1"][:], in_=sm["t1"][:]), S, S)
    tt("dve", sm["fir"][:], sm["fre"][:], sm["t1"][:], ALU.mult, S, S)
    tt("dve", sm["fii"][:], sm["fim"][:], sm["t1"][:], ALU.mult, S, S)
    ts("dve", sm["fii"][:], sm["fii"][:], -1.0, None, ALU.mult, None, S, S)

    Tc = sb("Tc", [128, NU, NT]); Ts = sb("Ts", [128, NU, NT]); tab_r = R()
    TB = (tab_r,)
    cp("dve", Tc[:, :, 0:1], sm["cs"][:].rearrange("q (u o) -> q u o", o=1), S, TB)
    cp("dve", Ts[:, :, 0:1], sm["sn"][:].rearrange("q (u o) -> q u o", o=1), S, TB)
    xflat = xT[:].rearrange("q c n -> q (c n)")
    HU = NU // 2
    tw = [xflat[:, i * HU * (NT // 2):(i + 1) * HU * (NT // 2)].rearrange("q (u n) -> q u n", u=HU) for i in range(2)]
    TB = (tab_r,) + tuple(xT_r)
    n = 1
    while n < NT:
        for u0 in (0, HU):
            us = slice(u0, u0 + HU)
            mc = Tc[:, us, n - 1:n].broadcast_to([128, HU, n])
            ms = Ts[:, us, n - 1:n].broadcast_to([128, HU, n])
            a, b = tw[0][:, :, 0:n], tw[1][:, :, 0:n]
            tt("dve", a, Tc[:, us, 0:n], mc, ALU.mult, TB, TB)
            tt("dve", b, Ts[:, us, 0:n], ms, ALU.mult, TB, TB)
            tt("dve", Tc[:, us, n:2 * n], a, b, ALU.subtract, TB, TB)
            tt("dve", a, Tc[:, us, 0:n], ms, ALU.mult, TB, TB)
            tt("dve", b, Ts[:, us, 0:n], mc, ALU.mult, TB, TB)
            tt("dve", Ts[:, us, n:2 * n], a, b, ALU.add, TB, TB)
        n *= 2
    TB = (tab_r,)
    decs = hT[:].rearrange("q c n -> q (c n)")[:, 0:2 * NU * 128].bitcast(F32).rearrange("q (u n) -> q u n", u=NU)
    decs_rs = tuple(hT_r)

    def build_decs():
        cp("dve", decs.rearrange("q u (s t) -> q u s t", t=8),
           sm["rho"][:].rearrange("q (u s t) -> q u s t", s=1, t=1).broadcast_to([128, NU, 16, 8]), S, decs_rs)
        memset("dve", decs.rearrange("q u (s t) -> q u s t", t=8)[:, :, :, 0:1], 0.0, decs_rs)

    Bwb = sb("Bwb", [96, 6, 2, 128], BF16); bw_r = R()
    memset("pool", Bwb[:], 0.0, (bw_r,))
    for u in range(NU):
        for g2 in range(2):
            for ri, src in enumerate((b_reT, b_imT)):
                ld(Bwb[32 * (u % 3) + 16 * g2:32 * (u % 3) + 16 * g2 + 16, u // 3, ri, 64 * g2:64 * g2 + 64],
                   src[2 * u + g2], (bw_r,), (bw_r,), q="pool")
    hflat = hT[:].rearrange("q c n -> q (c n)")
    def hview(i):
        return hflat[:, i * 512:(i + 1) * 512].bitcast(F32).rearrange("q (u c) -> q u c", u=NU)
    Cr = hview(0); Ci = hview(1); cw_r = R()
    with nc.allow_non_contiguous_dma(reason="ssm C load"):
        ld(Cr[:], c_reT.rearrange("(u g) n c -> g n u c", g=2).rearrange("g n u c -> (g n) u c"), (), (cw_r,))
        ld(Ci[:], c_imT.rearrange("(u g) n c -> g n u c", g=2).rearrange("g n u c -> (g n) u c"), (cw_r,), (cw_r,))
    Cw = sb("Cw", [128, NU, 2, 32], BF16)
    memset("pool", Cw[:], 0.0, (cw_r,))
    ct = [hview(2 + i) for i in range(3)]
    CS = (cw_r, ssm_r)
    frb = sm["fre"][:].rearrange("q (u o) -> q u o", o=1).broadcast_to([128, NU, 16])
    fib = sm["fim"][:].rearrange("q (u o) -> q u o", o=1).broadcast_to([128, NU, 16])
    tt("dve", ct[0][:], Cr[:], frb, ALU.mult, CS, (cw_r,))
    tt("dve", ct[1][:], Ci[:], fib, ALU.mult, CS, (cw_r,))
    tt("dve", ct[2][:], ct[0][:], ct[1][:], ALU.subtract, CS, (cw_r,))
    for g2 in range(2):
        cp("dve", Cw[64 * g2:64 * g2 + 64, :, 0, 16 * g2:16 * g2 + 16], ct[2][64 * g2:64 * g2 + 64, :, :],
           (cw_r,), (cw_r,))
    tt("dve", ct[0][:], Cr[:], fib, ALU.mult, CS, (cw_r,))
    tt("dve", ct[1][:], Ci[:], frb, ALU.mult, CS, (cw_r,))
    tt("dve", ct[2][:], ct[0][:], ct[1][:], ALU.add, CS, (cw_r,))
    for g2 in range(2):
        ts("dve", Cw[64 * g2:64 * g2 + 64, :, 1, 16 * g2:16 * g2 + 16], ct[2][64 * g2:64 * g2 + 64, :, :],
           -1.0, None, ALU.mult, None, (cw_r,), (cw_r,))

    carry = sb("carry", [128, NU, 2]); carry_rs = [R() for _ in range(NU)]
    memset("dve", carry[:], 0.0, tuple(carry_rs))
    SW = {nm: Ring(p, "sw_" + nm, 3 if nm in "ac" else 2, [128, NT], F32) for nm in ("a", "b", "c", "d", "gr", "gi")}
    HB = Ring(p, "hb", 2, [128, NT], BF16)
    h0_r = R(); fs_rs = [R() for _ in range(NU)]
    SMP = {}

    IL = [None, 1.0, 0.0]

    def tick(w=1.0):
        if IL[0] is not None:
            IL[2] += w * IL[1]
            while IL[2] >= 1.0 and IL[0] is not None:
                IL[2] -= 1.0
                if next(IL[0], None) is None:
                    IL[0] = None

    def drain():
        while IL[0] is not None:
            tick(1.0)
        IL[2] = 0.0

    def ssm_tile(N, sample, lite=False):
        for _ in ssm_gen(N, sample, lite):
            pass

    def ssm_gen(N, sample, lite=False):
        yacc = {}
        stt_ = {}
        st3 = {}

        def views(u):
            if sample:
                tcs = Tc[:, u, 0:8].rearrange("q (s t) -> q s t", s=1).broadcast_to([128, 16, 8])
                tss = Ts[:, u, 0:8].rearrange("q (s t) -> q s t", s=1).broadcast_to([128, 16, 8])
                v3 = lambda ap: ap.rearrange("q (s t) -> q s t", t=8)
            else:
                tcs, tss = Tc[:, u, 0:N], Ts[:, u, 0:N]
                v3 = lambda ap: ap
            return tcs, tss, v3

        def s1(u):
            ch, po = u // 3, 32 * (u % 3)
            tcs, tss, v3 = views(u)
            pre, prr = PS.get()
            mm(pre[:, 0:N], Bwb[po:po + 32, ch, 0, :], uTb[po:po + 32, ch, 0:N], True, True,
               (bw_r, uTb_r[ch]), (prr,))
            pim, pir = PS.get()
            mm(pim[:, 0:N], Bwb[po:po + 32, ch, 1, :], uTb[po:po + 32, ch, 0:N], True, True,
               (bw_r, uTb_r[ch]), (pir,))
            (a, ar), (b, br), (c, cr), (d, dr) = SW["a"].get(), SW["b"].get(), SW["c"].get(), SW["d"].get()
            tt("dve", v3(a[:, 0:N]), v3(pre[:, 0:N]), tcs, ALU.mult, (prr, tab_r), (ar,))
            tt("dve", v3(b[:, 0:N]), v3(pim[:, 0:N]), tss, ALU.mult, (pir, tab_r), (br,))
            tt("dve", v3(c[:, 0:N]), v3(pim[:, 0:N]), tcs, ALU.mult, (pir, tab_r), (cr,))
            tt("dve", v3(d[:, 0:N]), v3(pre[:, 0:N]), tss, ALU.mult, (prr, tab_r), (dr,))
            tt("dve", a[:, 0:N], a[:, 0:N], b[:, 0:N], ALU.add, (ar, br), (ar,))
            tt("dve", c[:, 0:N], c[:, 0:N], d[:, 0:N], ALU.subtract, (cr, dr), (cr,))
            stt_[u] = (a, ar, b, br, c, cr, d, dr)

        def s2(u):
            ch, po = u // 3, 32 * (u % 3)
            tcs, tss, v3 = views(u)
            a, ar, b, br, c, cr, d, dr = stt_[u]
            (gr, grr), (gi, gir) = SW["gr"].get(), SW["gi"].get()
            st3[u] = (gr, grr, gi, gir)
            if sample:
                rho_u = sm["rho"][:, u:u + 1]
                a0 = a[:, 0:N].rearrange("q (s t) -> q s t", t=8)[:, :, 0]
                c0 = c[:, 0:N].rearrange("q (s t) -> q s t", t=8)[:, :, 0]
                stt(a0, SMP["hpr"][:, u, :], rho_u, a0, ALU.mult, ALU.add, (h0_r, ssm_r, ar), (ar,))
                stt(c0, SMP["hpi"][:, u, :], rho_u, c0, ALU.mult, ALU.add, (h0_r, ssm_r, cr), (cr,))
                dec = decs[:, u, 0:N]
                dec_rs = decs_rs
                i_re, i_im = 0.0, 0.0
            else:
                dec = sm["rho"][:, u:u + 1].broadcast_to([128, N])
                dec_rs = ()
                i_re, i_im = carry[:, u, 0:1], carry[:, u, 1:2]
            cu = carry_rs[u]
            p.op("dve", lambda e, gr=gr, dec=dec, a=a, i_re=i_re: e.tensor_tensor_scan(
                out=gr[:, 0:N], data0=dec, data1=a[:, 0:N], initial=i_re, op0=ALU.mult, op1=ALU.add),
                (ar, tab_r, ssm_r, cu) + dec_rs, (grr,))
            p.op("dve", lambda e, gi=gi, dec=dec, c=c, i_im=i_im: e.tensor_tensor_scan(
                out=gi[:, 0:N], data0=dec, data1=c[:, 0:N], initial=i_im, op0=ALU.mult, op1=ALU.add),
                (cr, tab_r, ssm_r, cu) + dec_rs, (gir,))
            if lite:
                L = slice(N - 1, N)
                tt("pool", a[:, L], gr[:, L], Tc[:, u, L], ALU.mult, (grr, tab_r), (ar,))
                tt("pool", b[:, L], gi[:, L], Ts[:, u, L], ALU.mult, (gir, tab_r), (br,))
                tt("pool", c[:, L], gr[:, L], Ts[:, u, L], ALU.mult, (grr, tab_r), (cr,))
                tt("pool", d[:, L], gi[:, L], Tc[:, u, L], ALU.mult, (gir, tab_r), (dr,))
                tt("dve", carry[:, u, 0:1], a[:, L], b[:, L], ALU.subtract, (ar, br, grr, gir), (cu,))
                tt("dve", carry[:, u, 1:2], c[:, L], d[:, L], ALU.add, (cr, dr, grr, gir), (cu,))
                return
            tt("pool", v3(a[:, 0:N]), v3(gr[:, 0:N]), tcs, ALU.mult, (grr, tab_r), (ar,))
            tt("pool", v3(b[:, 0:N]), v3(gi[:, 0:N]), tss, ALU.mult, (gir, tab_r), (br,))
            tt("pool", v3(c[:, 0:N]), v3(gr[:, 0:N]), tss, ALU.mult, (grr, tab_r), (cr,))
            tt("pool", v3(d[:, 0:N]), v3(gi[:, 0:N]), tcs, ALU.mult, (gir, tab_r), (dr,))

        def s3(u):
            if lite:
                return
            ch, po = u // 3, 32 * (u % 3)
            a, ar, b, br, c, cr, d, dr = stt_.pop(u)
            gr, grr, gi, gir = st3.pop(u)
            cu = carry_rs[u]
            (hr, hrr), (hi, hir) = HB.get(), HB.get()
            tt("dve", hr[:, 0:N], a[:, 0:N], b[:, 0:N], ALU.subtract, (ar, br), (hrr,))
            tt("dve", hi[:, 0:N], c[:, 0:N], d[:, 0:N], ALU.add, (cr, dr), (hir,))
            if sample:
                l3 = lambda ap: ap[:, 0:N].rearrange("q (s t) -> q s t", t=8)[:, :, 7]
                tt("dve", SMP["fsr"][:, u, :], l3(a), l3(b), ALU.subtract, (ar, br), (fs_rs[u],))
                tt("dve", SMP["fsi"][:, u, :], l3(c), l3(d), ALU.add, (cr, dr), (fs_rs[u],))
            else:
                tt("dve", carry[:, u, 0:1], a[:, N - 1:N], b[:, N - 1:N], ALU.subtract, (ar, br, grr, gir), (cu,))
                tt("dve", carry[:, u, 1:2], c[:, N - 1:N], d[:, N - 1:N], ALU.add, (cr, dr, grr, gir), (cu,))
            if ch not in yacc:
                if sample:
                    yacc[ch] = PSO.get()
                else:
                    yacc[ch] = (spb_t[:, 0:256], SPBH)
            yp, ypr = yacc[ch]
            mm(yp[po:po + 32, 0:N], Cw[:, u, 0, :], hr[:, 0:N], True, False, (cw_r, hrr), (ypr,))
            mm(yp[po:po + 32, 0:N], Cw[:, u, 1, :], hi[:, 0:N], False, True, (cw_r, hir), (ypr,))
            if u % 3 == 2 or u == NU - 1:
                zr = 96 if ch < 5 else 32
                ta, tar = tmpA.get()
                stt(zT[0:zr, ch, 0:N], uTb[0:zr, ch, 0:N], dS[0:zr, ch:ch + 1], yp[0:zr, 0:N], ALU.mult, ALU.add,
                    (uTb_r[ch], dS_r, ypr), (zT_r[ch],))
                act(ta[0:zr, 0:N], zT[0:zr, ch, 0:N], AF.Square, (zT_r[ch],), (tar,))
                ts("dve", ta[0:zr, 0:N], ta[0:zr, 0:N], 0.044715, 1.0, ALU.mult, ALU.add, (tar,), (tar,))
                tt("dve", ta[0:zr, 0:N], ta[0:zr, 0:N], zT[0:zr, ch, 0:N], ALU.mult, (tar, zT_r[ch]), (tar,))
                act(ta[0:zr, 0:N], ta[0:zr, 0:N], AF.Sigmoid, (tar,), (tar,), scale=1.5957691216057308)
                tt("dve", zT[0:zr, ch, 0:N], zT[0:zr, ch, 0:N], ta[0:zr, 0:N], ALU.mult, (tar, zT_r[ch]), (zT_r[ch],))
                cp("pool", zTb[0:zr, ch, 0:N], zT[0:zr, ch, 0:N], (zT_r[ch],), (zTb_r[ch],))

        s1(0)
        for u in range(NU + 1):
            if u >= 1:
                s3(u - 1)
            if u < NU:
                s2(u)
            if u + 1 < NU:
                s1(u + 1)
            yield u

    def front(xsrc, psrc, t0, N, rcd, rsd, lite=False):
        nsub = N // 128
        for s in range(nsub):
            ld(xtok[:, s, :], xsrc[t0 + 128 * s:t0 + 128 * s + 128, :], (), (xtok_r[s],))
            if not lite:
                ld(ptok[:, s, :], psrc[t0 + 128 * s:t0 + 128 * s + 128, :], (), (ptok_r[s],))
        ld(rc[64:96, 0:N], rcd, (), (rc_r,))
        ld(rs_[64:96, 0:N], rsd, (), (rs_r,))
        for s in range(nsub):
            for c in range(8):
                pt, pr = PS.get()
                tr(pt[:, 0:128], xtok[:, s, 128 * c:128 * c + 128], ident[:], (xtok_r[s], ident_r), (pr,))
                cp("act" if c % 2 else "dve", xT[:, c, 128 * s:128 * s + 128], pt[:, 0:128], (pr,), (xT_r[c],))
            for c in range(0 if lite else 2):
                pt, pr = PS.get()
                tr(pt[:, 0:128], ptok[:, s, 128 * c:128 * c + 128], ident[:], (ptok_r[s], ident_r), (pr,))
                cp("act", pT[:, c, 128 * s:128 * s + 128], pt[:, 0:128], (pr,), (pT_r[c],))
        norm_fm([xT[:, c, 0:N] for c in range(8)], xT_r, 128, D, gA, gA_r,
                [xnT[:, c, 0:N] for c in range(8)], xnT_r, N)
        rhs = [xnT[:, c, 0:N] for c in range(8)]

        def epi_u(m, pt, pr, mr):
            cp("dve", uTb[0:mr, m, 0:N], pt[0:mr, 0:N], (pr,), (uTb_r[m],))
        linear_fm("w_in", C_U, 512, rhs, xnT_r, N, epi_u, mchunk=96)

        def epi_q(m, pt, pr, mr):
            cp("act", qlT[:, m, 0:N], pt[:, 0:N], (pr,), (qlT_r[m],))
        if not lite:
            linear_fm("w_in", C_Q, 384, rhs, xnT_r, N, epi_q)

        def epi_kv(m, pt, pr, mr):
            cp("act", kvT[:, m, 0:N], pt[:, 0:N], (pr,), (kvT_r[m],))
        linear_fm("w_in", C_KV, 256, rhs, xnT_r, N, epi_kv)

        def epi_ga(m, pt, pr, mr):
            act(sga[:, m, 0:N], pt[:, 0:N], AF.Sigmoid, (pr,), (sga_r[m],))
        if not lite:
            linear_fm("w_in", C_GA, 1024, rhs, xnT_r, N, epi_ga)

        def epi_gb(m, pt, pr, mr):
            act(sgb[:, m, 0:N], pt[:, 0:N], AF.Sigmoid, (pr,), (sgb_r[m],))
        if not lite:
            linear_fm("w_in", C_GB, 1024, rhs, xnT_r, N, epi_gb)
        wt, wtr = WR.get()
        wv = wt[:, 0:8 * 64].rearrange("q (k n) -> q k n", k=8)
        wb = W["w_in"][1]; wr_ = W["w_in"][2]
        ld(wv[:, :, 0:32], wb[:, C_KR:C_KR + 32].rearrange("(k q) n -> q k n", q=128), (wr_,), (wtr,))
        ld(wv[:, :, 32:48], wb[:, C_KR + 16:C_KR + 32].rearrange("(k q) n -> q k n", q=128), (wr_, wtr), (wtr,))
        ld(wv[:, :, 48:64], wb[:, C_KR:C_KR + 16].rearrange("(k q) n -> q k n", q=128), (wr_, wtr), (wtr,))
        for j, (dst, dr_) in enumerate(((kr1, kr1_r), (kr2, kr2_r))):
            pt, pr = PS.get()
            for k in range(8):
                mm(pt[64:96, 0:N], wv[:, k, 32 * j:32 * j + 32], rhs[k], k == 0, k == 7, (wtr, xnT_r[k]), (pr,))
            cp("act", dst[64:96, 0:N], pt[64:96, 0:N], (pr,), (dr_,))
        tt("dve", kr1[64:96, 0:N], kr1[64:96, 0:N], rc[64:96, 0:N], ALU.mult, (kr1_r, rc_r), (kr1_r,))
        tt("dve", kr2[64:96, 0:N], kr2[64:96, 0:N], rs_[64:96, 0:N], ALU.mult, (kr2_r, rs_r), (kr2_r,))
        tt("dve", kpeT[64:96, 0:N], kr1[64:96, 0:N], kr2[64:96, 0:N], ALU.add, (kr1_r, kr2_r), (kpeT_r,))
        if not lite:
            norm_fm([qlT[:, c, 0:N] for c in range(3)], qlT_r, 128, 384, gQ, gQ_r,
                    [qcT[:, c, 0:N] for c in range(3)], qcT_r, N)
        norm_fm([kvT[:, c, 0:N] for c in range(2)], kvT_r, 128, 256, gKV, gKV_r,
                [ckvT[:, c, 0:N] for c in range(2)], ckvT_r, N,
                out2=([ckvTb[:, c, 0:N] for c in range(2)], ckvTb_r))

    def head_norm(src, src_r, gain, gain_r, dst, dst_r, N, extra_scale=None):
        sq, sqr = SQ.get()
        act(sq[0:96, 0:N], src, AF.Square, (src_r,), (sqr,))
        pt, pr = PS.get()
        mm(pt[0:96, 0:N], ones[0:96, 0:96], sq[0:96, 0:N], True, True, (sqr, ones_r), (pr,))
        act(lnt[0:96, 0:N], pt[0:96, 0:N], AF.Ln, (pr, eps_r), (lnt_r,), scale=1.0 / 96, bias=epsT[0:96, :])
        act(rstd[0:96, 0:N], lnt[0:96, 0:N], AF.Exp, (lnt_r,), (rstd_r,), scale=-0.5)
        stt(dst, src, gain[0:96, 0:1], rstd[0:96, 0:N], ALU.mult, ALU.mult, (src_r, gain_r, rstd_r), (dst_r,))

    def q_heads(N):
        rhs = [qcT[:, c, 0:N] for c in range(3)]
        wb, wr_ = W["w_uq"][1], W["w_uq"][2]
        for h in range(8):
            tick()
            wt, wtr = WR.get()
            wv = wt[:, 0:3 * 128].rearrange("q (k n) -> q k n", k=3)
            b0 = 96 * h
            ld(wv[:, :, 0:96], wb[:, b0:b0 + 96].rearrange("(k q) n -> q k n", q=128), (wr_,), (wtr,))
            ld(wv[:, :, 96:112], wb[:, b0 + 80:b0 + 96].rearrange("(k q) n -> q k n", q=128), (wr_, wtr), (wtr,))
            ld(wv[:, :, 112:128], wb[:, b0 + 64:b0 + 80].rearrange("(k q) n -> q k n", q=128), (wr_, wtr), (wtr,))
            pt, pr = PS.get()
            for k in range(3):
                mm(pt[0:96, 0:N], wv[:, k, 0:96], rhs[k], k == 0, k == 2, (wtr, qcT_r[k]), (pr,))
            pt2, pr2 = PS.get()
            for k in range(3):
                mm(pt2[64:96, 0:N], wv[:, k, 96:128], rhs[k], k == 0, k == 2, (wtr, qcT_r[k]), (pr2,))
            qh, qhr = QH.get()
            cp("act", qh[0:64, 0:N], pt[0:64, 0:N], (pr,), (qhr,))
            tt("dve", qs1[64:96, 0:N], pt[64:96, 0:N], rc[64:96, 0:N], ALU.mult, (pr, rc_r), (qs1_r,))
            tt("dve", qs2[64:96, 0:N], pt2[64:96, 0:N], rs_[64:96, 0:N], ALU.mult, (pr2, rs_r), (qs2_r,))
            tt("dve", qh[64:96, 0:N], qs1[64:96, 0:N], qs2[64:96, 0:N], ALU.add, (qs1_r, qs2_r, qhr),
               (qhr,))
            head_norm(qh[:, 0:N], qhr, gqh, gqh_r, qhb[:, h, 0:N], qhb_r[h], N)

    def back(ydst, t0, N):
        nsub = N // 128
        crow = [96] * 5 + [32]

        def epi_glu(m, pt, pr, mr):
            ta, tar = tmpA.get()
            act(ta[0:mr, 0:N], pt[0:mr, 0:N], AF.Sigmoid, (pr,), (tar,))
            tt("dve", aoT[0:mr, m, 0:N], ta[0:mr, 0:N], zT[0:mr, m, 0:N], ALU.mult, (tar, zT_r[m]), (aoT_r[m],))
        linear_fm("w_glu", 0, 512, [zTb[0:crow[c], c, 0:N] for c in range(6)], zTb_r, N, epi_glu, krows=96, mchunk=96)

        def epi_a(m, pt, pr, mr):
            tt("dve", mgTb[:, m, 0:N], pt[:, 0:N], sga[:, m, 0:N], ALU.mult, (pr, sga_r[m]), (mgTb_r[m],))
        linear_fm("w_branch_a", 0, D, [aoT[0:crow[c], c, 0:N] for c in range(6)], aoT_r, N, epi_a, krows=96)

        def epi_b(m, pt, pr, mr):
            ta, tar = tmpA.get()
            tt("dve", ta[:, 0:N], pt[:, 0:N], sgb[:, m, 0:N], ALU.mult, (pr, sgb_r[m]), (tar,))
            tt("pool", mgTb[:, m, 0:N], ta[:, 0:N], mgTb[:, m, 0:N], ALU.add, (tar, mgTb_r[m]), (mgTb_r[m],))
        linear_fm("w_branch_b", 0, D, [boT[:, c, 0:N] for c in range(4)], boT_r, N, epi_b)

        def epi_o(m, pt, pr, mr):
            tt("dve", xT[:, m, 0:N], pt[:, 0:N], xT[:, m, 0:N], ALU.add, (pr, xT_r[m]), (xT_r[m],))
        linear_fm("w_out", 0, D, [mgTb[:, c, 0:N] for c in range(8)], mgTb_r, N, epi_o)
        norm_fm([xT[:, c, 0:N] for c in range(8)], xT_r, 128, D, gF, gF_r,
                [xnT[:, c, 0:N] for c in range(8)], xnT_r, N)
        rhs = [xnT[:, c, 0:N] for c in range(8)]

        def epi_g(m, pt, pr, mr):
            ta, tar = tmpA.get()
            act(ta[:, 0:N], pt[:, 0:N], AF.Sigmoid, (pr,), (tar,))
            tt("dve", hT[:, m, 0:N], pt[:, 0:N], ta[:, 0:N], ALU.mult, (pr, tar), (hT_r[m],))
        linear_fm("w_gate", 0, D_FF, rhs, xnT_r, N, epi_g)

        def epi_up(m, pt, pr, mr):
            tt("dve", hT[:, m, 0:N], pt[:, 0:N], hT[:, m, 0:N], ALU.mult, (pr, hT_r[m]), (hT_r[m],))
        linear_fm("w_up", 0, D_FF, rhs, xnT_r, N, epi_up)
        linear_fm("w_down", 0, D, [hT[:, c, 0:N] for c in range(22)], hT_r, N, epi_o)
        norm_fm([xT[:, c, 0:N] for c in range(8)], xT_r, 128, D, gP, gP_r,
                [xnT[:, c, 0:N] for c in range(8)], xnT_r, N)

        def epi_pg(m, pt, pr, mr):
            act(gtT[:, m, 0:N], pt[:, 0:N], AF.Sigmoid, (pr,), (gtT_r[m],))
        linear_fm("w_ple_gate", 0, D, rhs, xnT_r, N, epi_pg)

        def epi_pl(m, pt, pr, mr):
            ta, tar = tmpA.get()
            tt("dve", ta[:, 0:N], pt[:, 0:N], gtT[:, m, 0:N], ALU.mult, (pr, gtT_r[m]), (tar,))
            tt("pool", xT[:, m, 0:N], ta[:, 0:N], xT[:, m, 0:N], ALU.add, (tar, xT_r[m]), (xT_r[m],))
        linear_fm("w_ple", 0, D, [pT[:, c, 0:N] for c in range(2)], pT_r, N, epi_pl)
        for s in range(nsub):
            for c in range(8):
                pt, pr = PS.get()
                tr(pt[:, 0:128], xT[:, c, 128 * s:128 * s + 128], ident[:], (xT_r[c], ident_r), (pr,))
                cp("act" if c % 2 else "dve", xtok[:, s, 128 * c:128 * c + 128], pt[:, 0:128], (pr,), (xtok_r[s],))
            ld(ydst[t0 + 128 * s:t0 + 128 * s + 128, :], xtok[:, s, :], (xtok_r[s],), ())

    def store_ckv_kpe(cdst, kdst, t0, N):
        nsub = N // 128
        for s in range(nsub):
            for c in range(2):
                pt, pr = PS.get()
                tr(pt[:, 0:128], ckvT[:, c, 128 * s:128 * s + 128], ident[:], (ckvT_r[c], ident_r), (pr,))
                cp("act", otok[:, s, 128 * c:128 * c + 128], pt[:, 0:128], (pr,), (otok_r[s],))
            ld(cdst[t0 + 128 * s:t0 + 128 * s + 128, :], otok[:, s, :], (otok_r[s],), ())
            pt, pr = PS.get()
            tr(pt[:, 0:32], kpeT[64:96, 128 * s:128 * s + 128], ident[64:96, 64:96], (kpeT_r, ident_r), (pr,))
            cp("act", ktok[:, s, :], pt[:, 0:32], (pr,), (ktok_r[s],))
            ld(kdst[t0 + 128 * s:t0 + 128 * s + 128, :], ktok[:, s, :], (ktok_r[s],), ())

    arena.reset()
    KB = Ring(p, "kb", 2, [96, KSEQ], BF16, arena=arena)
    VB = Ring(p, "vb", 2, [128, KSEQ // 128, 66], BF16, arena=arena)
    PB = Ring(p, "pb", 3, [128, NT], BF16, arena=arena)
    botok = arena.get([128, 2, 512], BF16); botok_r = [R(), R()]
    rec = arena.get([128, 4], F32); rec_r = R()

    def prompt_kv(t, N):
        rhs = [ckvTb[:, c, 0:N] for c in range(2)]
        wb, wr_ = W["w_uk"][1], W["w_uk"][2]
        wt, wtr = WR.get()
        wv = wt[:, 0:2 * 512].rearrange("q (k n) -> q k n", k=2)
        ld(wv, wb.rearrange("(k q) n -> q k n", q=128), (wr_,), (wtr,))
        for h in range(8):
            tick()
            pt, pr = PS.get()
            for k in range(2):
                mm(pt[0:64, 0:N], wv[:, k, 64 * h:64 * h + 64], rhs[k], k == 0, k == 1, (wtr, ckvTb_r[k]), (pr,))
            cp("act", kh[0:64, 0:N], pt[0:64, 0:N], (pr,), (kh_r,))
            cp("pool", kh[64:96, 0:N], kpeT[64:96, 0:N], (kpeT_r, kh_r), (kh_r,))
            khb_, khb_r_ = khb, khb_r
            head_norm(kh[:, 0:N], kh_r, gkh, gkh_r, khb_[:, 0:N], khb_r_, N)
            ld(Kscr[h, :, t * NT:t * NT + N], khb_[:, 0:N], (khb_r_,), (Kscr_r,))
        wb, wr_ = W["w_uv"][1], W["w_uv"][2]
        wt, wtr = WR.get()
        wv = wt[:, 0:2 * 512].rearrange("q (k n) -> q k n", k=2)
        ld(wv, wb.rearrange("(k q) n -> q k n", q=128), (wr_,), (wtr,))
        for s in range(N // 128):
            pt, pr = PS.get()
            for k in range(2):
                mm(pt[:, 0:512], ckvTb[:, k, 128 * s:128 * s + 128], wv[:, k, :], k == 0, k == 1,
                   (wtr, ckvTb_r[k]), (pr,))
            cp("act", vtok[:, s, :, 0:64], pt[:, 0:512].rearrange("q (h v) -> q h v", h=8), (pr,), (vtok_r[s],))
            ld(Vscr[(t * NT) // 128 + s], vtok[:, s, :, :].rearrange("q h v -> q (h v)"), (vtok_r[s],), (Vscr_r,))

    def prompt_attn(t, N):
        t = t + NPRE
        nkb = (t * NT + N) // 128
        nsub = N // 128
        for h in range(8):
            tick()
            kb, kbr = KB.get()
            ld(kb[:, 0:nkb * 128], Kscr[h, :, 0:nkb * 128], (Kscr_r,), (kbr,))
            vb, vbr = VB.get()
            ld(vb[:, 0:nkb, 0:65], Vscr[0:nkb, :, 65 * h:65 * h + 65].rearrange("b q v -> q b v"), (Vscr_r,), (vbr,))
            op_, opr = PSO.get()
            accs = [(op_, opr)] if nsub == 1 else [(op_, opr), PSO.get()]
            def pv(j, pb, pbr, q0):
                for s in range(nsub):
                    if 128 * s < q0:
                        continue
                    jl = (t * NT) // 128 + s
                    mm(accs[s][0][:, 0:65], pb[:, 128 * s:128 * s + 128], vb[:, j, 0:65], j == 0, j == jl,
                       (pbr, vbr), (accs[s][1],))
            pend = None
            for j in range(nkb):
                q0 = max(0, j - (t * NT) // 128) * 128
                st_, str_ = PS.get()
                mm(st_[:, q0:N], kb[:, 128 * j:128 * j + 128], qhb[:, h, q0:N], True, True, (kbr, qhb_r[h]), (str_,))
                pb, pbr = PB.get()
                if j < 2 * NPRE:
                    act(pb[:, q0:N], st_[:, q0:N], AF.Exp, (str_, pmask_r), (pbr,), scale=ATTN_SCALE, bias=pmask[:, 0:1])
                else:
                    act(pb[:, q0:N], st_[:, q0:N], AF.Exp, (str_,), (pbr,), scale=ATTN_SCALE)
                if 128 * j >= t * NT:
                    tt("pool", pb[:, q0:q0 + 128], pb[:, q0:q0 + 128], tri[:], ALU.mult, (pbr, tri_r), (pbr,))
                if pend is not None:
                    pv(*pend)
                pend = (j, pb, pbr, q0)
            pv(*pend)
            for s in range(nsub):
                oa, oar = accs[s]
                p.op("dve", lambda e, oa=oa, s=s: e.reciprocal(out=rec[:, s:s + 1], in_=oa[:, 64:65]), (oar,), (rec_r,))
                ts("dve", botok[:, s, 64 * h:64 * h + 64], oa[:, 0:64], rec[:, s:s + 1], None, ALU.mult, None,
                   (oar, rec_r), (botok_r[s],))
        for s in range(nsub):
            for c in range(4):
                pt, pr = PS.get()
                ptb = pt[:].bitcast(BF16)
                tr(ptb[:, 0:128], botok[:, s, 128 * c:128 * c + 128], identb[:], (botok_r[s], identb_r), (pr,))
                cp("act", boT[:, c, 128 * s:128 * s + 128], ptb[:, 0:128], (pr,), (boT_r[c],))

    arena.reset()
    NG = NPG // 4
    idxc = arena.get([128, 16, NG], I32); idx_r = R()
    goff = arena.get([128, 2], I32)
    PG4 = Ring(p, "pg4", 4, [128, 1024], BF16, arena=arena)
    PK4 = Ring(p, "pk4", 4, [128, 128], BF16, arena=arena)
    PGT = Ring(p, "pgt", 3, [128, 256], BF16, arena=arena)
    PKT = Ring(p, "pkt", 3, [32, 128], BF16, arena=arena)
    KSQ = Ring(p, "ksq", 2, [128, 512], BF16, arena=arena)
    KS2 = Ring(p, "ks2", 2, [128, 128], BF16, arena=arena)
    KSS = Ring(p, "kss", 4, [128, 4], F32, arena=arena)
    SS = Ring(p, "ss", 6, [128, 12], F32, arena=arena)
    SC = Ring(p, "sc", 5, [128, 64], F32, arena=arena)
    PP = Ring(p, "pp", 4, [128, 64], BF16, arena=arena)
    Qab = arena.get([128, 2, 16, 64], BF16); Qab_r = R()
    Qrp = arena.get([32, 16, 64], BF16); Qrp_r = R()
    qg = arena.get([96, 128], BF16); qg_r = R()
    OnT = arena.get([128, 2, 16, 64], BF16); OnT_r = R()
    On = arena.get([64, 256], BF16); On_r = R()
    rec2 = arena.get([64, 2], F32); rec2_r = R()
    ctokb = arena.get([128, 256], BF16); ctokb_r = R()
    kpeTb = arena.get([32, 128], BF16); kpeTb_r = R()
    kpetok = arena.get([128, 32], BF16); kpetok_r = R()
    kssn = arena.get([128, 2], F32); kssn_r = R()
    wukS = arena.get([128, 2, 512], BF16); wukS_r = R()
    wukT = arena.get([64, 8, 256], BF16); wukT_r = R()
    wuvS = arena.get([128, 2, 512], BF16); wuvS_r = R()
    for nm in ("h0r", "h0i", "hpr", "hpi"):
        SMP[nm] = arena.get([128, NU, 16], F32)
    SMP["fsr"], SMP["fsi"] = SMP["h0r"], SMP["h0i"]
    h0r, h0i, hpr, hpi, fsr, fsi = (SMP[k] for k in ("h0r", "h0i", "hpr", "hpi", "fsr", "fsi"))
    sttok = xtok[0:16, :, :].rearrange("q s d -> q (s d)")
    sttok_rs = (xtok_r[0], xtok_r[1])

    def sample_setup():
        p.op("pool", lambda e: e.iota(goff[:, 0:1], [[0, 1]], base=0, channel_multiplier=1), (), (idx_r,))
        p.op("dve", lambda e: e.tensor_single_scalar(out=goff[:, 0:1], in_=goff[:, 0:1], scalar=31, op=ALU.bitwise_and),
             (idx_r,), (idx_r,))

    def sample_ptab(st):
        src = ptab[16 * st:16 * st + 16, :].rearrange("s (m a) -> a s m", a=4)
        for a in range(4):
            ld(idxc[32 * a:32 * a + 32, :, :], src[a].partition_broadcast(32), (idx_r,), (idx_r,), q="pool")
        ts("pool", idxc[:], idxc[:], 32, None, ALU.mult, None, (idx_r,), (idx_r,))
        tt("pool", idxc[:], idxc[:], goff[:, 0:1].rearrange("q (a b) -> q a b", b=1).broadcast_to([128, 16, NG]),
           ALU.add, (idx_r,), (idx_r,))

    def sample_attn(st):
        N = 128
        for h in range(8):
            ts("dve", qg[0:96, :], qhb[:, h, 0:N], gkh[:, 0:1], None, ALU.mult, None, (qhb_r[h], gkh_r, qg_r), (qg_r,))
            for k in range(2):
                pt, pr = PS.get()
                mm(pt[:, 0:N], wukT[:, h, 128 * k:128 * k + 128], qg[0:64, :], True, True, (wukT_r, qg_r), (pr,))
                cp("act", Qab[:, k, :, 8 * h:8 * h + 8], pt[:, 0:N].rearrange("q (s t) -> q s t", t=8), (pr, Qab_r),
                   (Qab_r,))
            pt, pr = PS.get()
            mm(pt[0:32, 0:N], identb[64:96, 64:96], qg[64:96, :], True, True, (identb_r, qg_r), (pr,))
            cp("act", Qrp[:, :, 8 * h:8 * h + 8], pt[0:32, 0:N].rearrange("q (s t) -> q s t", t=8), (pr, Qrp_r),
               (Qrp_r,))
        for c in range(2):
            pt, pr = PS.get()
            ptb = pt[:].bitcast(BF16)
            tr(ptb[:, 0:128], ckvTb[:, c, 0:N], identb[:], (ckvTb_r[c], identb_r), (pr,))
            cp("act", ctokb[:, 128 * c:128 * c + 128], ptb[:, 0:128], (pr, ctokb_r), (ctokb_r,))
        pt, pr = PS.get()
        mm(pt[0:32, 0:N], ident[64:96, 64:96], kpeT[64:96, 0:N], True, True, (ident_r, kpeT_r), (pr,))
        cp("act", kpeTb[:, :], pt[0:32, 0:N], (pr,), (kpeTb_r,))
        pt, pr = PS.get()
        tr(pt[:, 0:32], kpeT[64:96, 0:N], ident[64:96, 64:96], (kpeT_r, ident_r), (pr,))
        cp("act", kpetok[:, :], pt[:, 0:32], (pr,), (kpetok_r,))

        ks2, ks2r = KS2.get()
        act(ks2[:, 0:32], kpetok[:, :], AF.Square, (kpetok_r,), (ks2r,))
        p.op("dve", lambda e, ks2=ks2: e.tensor_reduce(out=kssn[:, 0:1], in_=ks2[:, 0:32], axis=AX.X, op=ALU.add),
             (ks2r,), (kssn_r,))
        gathers = [(b, m) for b in range(16) for m in range(NG)]
        gbuf = {}

        def issue_gather(gi):
            if gi >= len(gathers) or gi in gbuf:
                return
            b, m = gathers[gi]
            pg, pgr = PG4.get()
            pk, pkr = PK4.get()
            p.dma("pool", lambda e, pg=pg, b=b, m=m: e.indirect_dma_start(
                out=pg[:, :], out_offset=None, in_=cache_ckv,
                in_offset=bass.IndirectOffsetOnAxis(ap=idxc[:, b, m:m + 1].bitcast(U32), axis=0)),
                (idx_r,), (pgr,))
            p.dma("pool", lambda e, pk=pk, b=b, m=m: e.indirect_dma_start(
                out=pk[:, :], out_offset=None, in_=cache_kpe,
                in_offset=bass.IndirectOffsetOnAxis(ap=idxc[:, b, m:m + 1].bitcast(U32), axis=0)),
                (idx_r,), (pkr,))
            ks2, ks2r = KS2.get()
            act(ks2[:, :], pk[:, :], AF.Square, (pkr,), (ks2r,))
            kss, kssr = KSS.get()
            p.op("dve", lambda e, kss=kss, ks2=ks2: e.tensor_reduce(
                out=kss[:, 0:4], in_=ks2[:, :].rearrange("q (c d) -> q c d", c=4), axis=AX.X, op=ALU.add), (ks2r,), (kssr,))
            gbuf[gi] = (pg, pgr, pk, pkr, kss, kssr)

        blocks = []
        for b in range(16):
            for m in range(NG):
                for c in range(4):
                    blocks.append(dict(b=b, gi=b * NG + m, c=c, new=False, first=(m == 0 and c == 0), last=False))
            blocks.append(dict(b=b, gi=None, c=0, new=True, first=(NG == 0), last=True))
        accs = {}

        def stage_a(bl):
            b = bl["b"]
            if bl["first"]:
                accs[b] = PSO.get()
            if bl["new"]:
                bl["tok"], bl["tok_r"] = ctokb[:, :], (ctokb_r,)
                bl["T"], bl["T_r"] = [ckvTb[:, 0, 0:N], ckvTb[:, 1, 0:N]], tuple(ckvTb_r)
                bl["kT"], bl["kT_r"] = kpeTb[:, :], (kpeTb_r,)
                bl["ktok"], bl["ktok_r"] = kpetok[:, :], (kpetok_r,)
                bl["kss"], bl["kss_r"] = kssn[:, 0:1], kssn_r
            else:
                gi, c = bl["gi"], bl["c"]
                if c == 0:
                    issue_gather(gi); issue_gather(gi + 1); issue_gather(gi + 2)
                pg, pgr, pk, pkr, kss, kssr = gbuf[gi]
                bl["kss"], bl["kss_r"] = kss[:, c:c + 1], kssr
                bl["tok"], bl["tok_r"] = pg[:, 256 * c:256 * c + 256], (pgr,)
                bl["ktok"], bl["ktok_r"] = pk[:, 32 * c:32 * c + 32], (pkr,)
                import os
                DBG2 = os.environ.get("KDBG", "")
                if "notr" in DBG2:
                    bl["T"], bl["T_r"] = [ckvTb[:, 0, 0:N], ckvTb[:, 1, 0:N]], tuple(ckvTb_r)
                    bl["kT"], bl["kT_r"] = kpeTb[:, :], (kpeTb_r,)
                else:
                    pt, pr = PS.get()
                    ptb = pt[:].bitcast(BF16)
                    for k in range(2):
                        tr(ptb[:, 128 * k:128 * k + 128], pg[:, 256 * c + 128 * k:256 * c + 128 * k + 128], identb[:],
                           (pgr, identb_r), (pr,))
                    pgT, pgTr = PGT.get()
                    cp("dve", pgT[:, :], ptb[:, 0:256], (pr,), (pgTr,))
                    pt2, pr2 = PS.get()
                    ptb2 = pt2[:].bitcast(BF16)
                    tr(ptb2[0:32, 0:128], pk[:, 32 * c:32 * c + 32], identb[:], (pkr, identb_r), (pr2,))
                    pkT, pkTr = PKT.get()
                    cp("dve", pkT[:, :], ptb2[0:32, 0:128], (pr2,), (pkTr,))
                    bl["T"], bl["T_r"] = [pgT[:, 0:128], pgT[:, 128:256]], (pgTr,)
                    bl["kT"], bl["kT_r"] = pkT[:, :], (pkTr,)
            kp, kpr = PS.get()
            for k in range(2):
                mm(kp[:, 0:512], bl["T"][k], wukS[:, k, :], k == 0, k == 1, bl["T_r"] + (wukS_r,), (kpr,))
            sp_, spr = SPB.get()
            for k in range(2):
                mm(sp_[:, 0:64], bl["T"][k], Qab[:, k, b, :], k == 0, False, bl["T_r"] + (Qab_r,), (spr,))
            mm(sp_[:, 0:64], bl["kT"], Qrp[:, b, :], False, True, bl["kT_r"] + (Qrp_r,), (spr,))
            bl["sp"], bl["sp_r"] = sp_, spr
            ksq, ksqr = KSQ.get()
            act(ksq[:], kp[:, 0:512], AF.Square, (kpr,), (ksqr,))
            ss, ssr = SS.get()
            p.op("dve", lambda e, ss=ss, ksq=ksq: e.tensor_reduce(
                out=ss[:, 0:8], in_=ksq[:].rearrange("q (h d) -> q h d", h=8), axis=AX.X, op=ALU.add), (ksqr,), (ssr,))
            ts("dve", ss[:, 0:8], ss[:, 0:8], bl["kss"], 1.0 / 96, ALU.add, ALU.mult, (ssr, bl["kss_r"]), (ssr,))
            bl["ss"], bl["ss_r"] = ss, ssr

        def stage_b(bl):
            ss, ssr = bl["ss"], bl["ss_r"]
            act(ss[:, 0:8], ss[:, 0:8], AF.Ln, (ssr, eps_r), (ssr,), bias=epsT[:, :])
            act(ss[:, 0:8], ss[:, 0:8], AF.Exp, (ssr,), (ssr,), scale=-0.5)
            sc, scr = SC.get()
            tt("dve", sc[:].rearrange("q (h t) -> q h t", h=8), bl["sp"][:, 0:64].rearrange("q (h t) -> q h t", h=8),
               ss[:, 0:8].rearrange("q (h o) -> q h o", o=1).broadcast_to([128, 8, 8]), ALU.mult, (bl["sp_r"], ssr), (scr,))
            bl["sc"], bl["sc_r"] = sc, scr

        def stage_c(bl):
            b = bl["b"]
            sc, scr = bl["sc"], bl["sc_r"]
            pp, ppr = PP.get()
            if not bl["new"]:
                act(pp[:], sc[:], AF.Exp, (scr,), (ppr,), scale=ATTN_SCALE)
            else:
                act(sc[:], sc[:], AF.Exp, (scr,), (scr,), scale=ATTN_SCALE)
                tt("dve", pp[:], sc[:], masks[:, 64 * b:64 * b + 64], ALU.mult, (scr, masks_r), (ppr,))
            acc, acc_r = accs[b]
            mm(acc[0:64, 0:256], pp[:], bl["tok"], bl["first"], False, (ppr,) + bl["tok_r"], (acc_r,))
            mm(acc[0:64, 256:258], pp[:], ones[:, 0:2], False, bl["last"], (ppr, ones_r), (acc_r,))
            if bl["last"]:
                p.op("dve", lambda e, acc=acc: e.reciprocal(out=rec2[:, 0:1], in_=acc[0:64, 256:257]), (acc_r,), (rec2_r,))
                ts("dve", On[:, :], acc[0:64, 0:256], rec2[:, 0:1], None, ALU.mult, None, (acc_r, rec2_r), (On_r,))
                for c in range(2):
                    pt, pr = PS.get()
                    ptb = pt[:].bitcast(BF16)
                    tr(ptb[:, 0:64], On[:, 128 * c:128 * c + 128], identb[0:64, 0:64], (On_r, identb_r), (pr,))
                    cp("act", OnT[:, c, b, :], ptb[:, 0:64], (pr, OnT_r), (OnT_r,))

        nb = len(blocks)
        import os
        DBG = os.environ.get("KDBG", "")
        if "noblocks" in DBG:
            nb = 0
            blocks = []
        if "onlynew" in DBG:
            blocks = [dict(bl, first=True) for bl in blocks if bl["new"]]
            nb = len(blocks)
        L1, L2 = (0, 0) if "nolag" in DBG else (1, 3)
        for i in range(nb + 4):
            if i < nb:
                stage_a(blocks[i])
            if 0 <= i - L1 < nb:
                stage_b(blocks[i - L1])
            if 0 <= i - L2 < nb:
                stage_c(blocks[i - L2])
        for h in range(8):
            pt, pr = PS.get()
            po = 64 * (h % 2)
            for k in range(2):
                mm(pt[po:po + 64, 0:N], wuvS[:, k, 64 * h:64 * h + 64],
                   OnT[:, k, :, 8 * h:8 * h + 8], k == 0, k == 1, (wuvS_r, OnT_r), (pr,))
            cp("act", boT[po:po + 64, h // 2, 0:N], pt[po:po + 64, 0:N], (pr, boT_r[h // 2]), (boT_r[h // 2],))

    for t in range(NPRE):
        front(x_pre, None, t * NT, NT, ropec_pre[:, t * NT:(t + 1) * NT], ropes_pre[:, t * NT:(t + 1) * NT], lite=True)
        IL[0], IL[1] = ssm_gen(NT, False, lite=True), 2.0
        prompt_kv(t, NT)
        drain()
    for t in range(NPT):
        front(x_p, p_p, t * NT, NT, ropec_p[:, t * NT:(t + 1) * NT], ropes_p[:, t * NT:(t + 1) * NT])
        store_ckv_kpe(ckv_p, kpe_p, t * NT, NT)
        IL[0], IL[1] = ssm_gen(NT, False), 0.5
        q_heads(NT)
        prompt_kv(NPRE + t, NT)
        IL[1] = 1.0
        prompt_attn(t, NT)
        drain()
        back(y_p, t * NT, NT)
    fo = sb("fo", [128, NU, 2]); fo_r = R()
    t1_, t2_ = sm["t1"], sm["t2"]
    tt("dve", t1_[:], carry[:, :, 0], sm["fre"][:], ALU.mult, tuple(carry_rs) + (ssm_r,), S)
    tt("dve", t2_[:], carry[:, :, 1], sm["fim"][:], ALU.mult, tuple(carry_rs) + (ssm_r,), S)
    tt("dve", fo[:, :, 0], t1_[:], t2_[:], ALU.subtract, S, (fo_r,))
    tt("dve", t1_[:], carry[:, :, 0], sm["fim"][:], ALU.mult, tuple(carry_rs) + (ssm_r,), S)
    tt("dve", t2_[:], carry[:, :, 1], sm["fre"][:], ALU.mult, tuple(carry_rs) + (ssm_r,), S)
    tt("dve", fo[:, :, 1], t1_[:], t2_[:], ALU.add, S, (fo_r,))
    with nc.allow_non_contiguous_dma(reason="final state store"):
        for ri, dst in enumerate((sre_p, sim_p)):
            ld(dst.rearrange("o (u g n) -> o g n u", u=NU, g=2)[0].rearrange("g n u -> (g n) u"), fo[:, :, ri],
               (fo_r,), ())

    p.barrier()
    if NST > 0:
        sample_setup()
        ld(wukS[:], W["w_uk"][1].rearrange("(k q) n -> q k n", q=128), (W["w_uk"][2],), (wukS_r,))
        ld(wuvS[:], W["w_uv"][1].rearrange("(k q) n -> q k n", q=128), (W["w_uv"][2],), (wuvS_r,))
        ld(wukT[:], W["w_ukT"][1].rearrange("(h d) c -> d h c", h=8), (W["w_ukT"][2],), (wukT_r,))
    for st in range(NST):
        for ri, (src, dst) in enumerate(((st_re, h0r), (st_im, h0i))):
            ld(sttok[:, :], src[16 * st:16 * st + 16, :], (), sttok_rs)
            for u in range(NU):
                pt, pr = PS.get()
                tr(pt[:, 0:16], sttok[:, 128 * u:128 * u + 128], ident[0:16, 0:16], sttok_rs + (ident_r,), (pr,))
                cp("act", dst[:, u, :], pt[:, 0:16], (pr, h0_r), (h0_r,))
        b16 = lambda ap: ap.rearrange("q (u o) -> q u o", o=1).broadcast_to([128, NU, 16])
        H0 = (h0_r, ssm_r)
        tt("dve", ct[0][:], h0r[:], b16(sm["fir"][:]), ALU.mult, H0, (cw_r,))
        tt("dve", ct[1][:], h0i[:], b16(sm["fii"][:]), ALU.mult, H0, (cw_r,))
        tt("dve", hpr[:], ct[0][:], ct[1][:], ALU.subtract, (cw_r,), (h0_r,))
        tt("dve", ct[0][:], h0r[:], b16(sm["fii"][:]), ALU.mult, H0, (cw_r,))
        tt("dve", ct[1][:], h0i[:], b16(sm["fir"][:]), ALU.mult, H0, (cw_r,))
        tt("dve", hpi[:], ct[0][:], ct[1][:], ALU.add, (cw_r,), (h0_r,))
        sample_ptab(st)
        build_decs()
        front(x_s, p_s, st * 128, 128, ropec_s, ropes_s)
        store_ckv_kpe(ckv_s, kpe_s, st * 128, 128)
        ssm_tile(128, True)
        FS = tuple(fs_rs) + (ssm_r,)
        tt("dve", ct[0][:], fsr[:], b16(sm["fre"][:]), ALU.mult, FS, (cw_r,))
        tt("dve", ct[1][:], fsi[:], b16(sm["fim"][:]), ALU.mult, FS, (cw_r,))
        tt("dve", hpr[:], ct[0][:], ct[1][:], ALU.subtract, (cw_r,), (h0_r,))
        tt("dve", ct[0][:], fsr[:], b16(sm["fim"][:]), ALU.mult, FS, (cw_r,))
        tt("dve", ct[1][:], fsi[:], b16(sm["fre"][:]), ALU.mult, FS, (cw_r,))
        tt("dve", hpi[:], ct[0][:], ct[1][:], ALU.add, (cw_r,), (h0_r,))
        for ri, (src, dst) in enumerate(((hpr, sre_s), (hpi, sim_s))):
            for u in range(NU):
                pt, pr = PS.get()
                tr(pt[0:16, 0:128], src[:, u, :], ident[:], (h0_r, ident_r), (pr,))
                cp("act", sttok[:, 128 * u:128 * u + 128], pt[0:16, 0:128], (pr,) + sttok_rs, sttok_rs)
            ld(dst[16 * st:16 * st + 16, :], sttok[:, :], sttok_rs, ())
        q_heads(128)
        sample_attn(st)
        back(y_s, st * 128, 128)

    p.finish()
    return nc, es


def _prep(cfg, inp, core, ncores):
    SEQ, NS, NPOOL, NPG = cfg["SEQ"], cfg["NS"], cfg["NPOOL"], cfg["NPG"]
    NPRE = cfg.get("NPRE", 0)
    f = lambda a: np.ascontiguousarray(np.asarray(a, dtype=np.float32))
    m = {}
    if NPRE:
        sq, half = core // 2, core % 2
        pos0 = half * SEQ
        m["x_p"] = f(inp["x_prompt"][sq, pos0:pos0 + SEQ]); m["p_p"] = f(inp["p_prompt"][0, sq, pos0:pos0 + SEQ])
        m["x_pre"] = f(inp["x_prompt"][sq, 0:SEQ]) if half else np.zeros((SEQ, D), np.float32)
        m["pmask"] = np.full((128, 1), 0.0 if half else -30000.0, np.float32)
    else:
        pos0 = 0
        m["x_p"] = f(inp["x_prompt"][core]); m["p_p"] = f(inp["p_prompt"][0, core])
        m["pmask"] = np.zeros((128, 1), np.float32)
    sl = slice(core * NS, (core + 1) * NS)
    m["x_s"] = f(inp["x_sample"][sl]).reshape(NS * 8, D); m["p_s"] = f(inp["p_sample"][0, sl]).reshape(NS * 8, PLE)
    m["cache_ckv"] = f(inp["cache_ckv"][0]).reshape(NPOOL * 32, 1024)
    m["cache_kpe"] = f(inp["cache_kpe"][0]).reshape(NPOOL * 32, 128)
    m["st_re"] = f(inp["state_ssm_re"][0, sl]).reshape(NS, 2048)
    m["st_im"] = f(inp["state_ssm_im"][0, sl]).reshape(NS, 2048)
    m["ptab"] = np.ascontiguousarray(np.asarray(inp["page_table"][sl], dtype=np.int32))
    m["w_in"] = f(inp["w_in"][0]); m["w_uq"] = f(inp["w_uq"][0]).reshape(384, 768)
    m["w_uk"] = f(inp["w_uk"][0]).reshape(256, 512); m["w_uv"] = f(inp["w_uv"][0]).reshape(256, 512)
    m["w_ukT"] = f(np.transpose(np.asarray(inp["w_uk"][0]), (1, 2, 0))).reshape(512, 256)
    for nm in ("w_glu", "w_branch_a", "w_branch_b", "w_out", "w_gate", "w_up", "w_down", "w_ple_gate", "w_ple"):
        m[nm] = f(inp[nm][0])
    m["g_attn"] = f(inp["norm_attn_g"][0]); m["g_ffn"] = f(inp["norm_ffn_g"][0]); m["g_ple"] = f(inp["norm_ple_g"][0])
    m["g_q"] = f(inp["q_norm_g"][0]); m["g_kv"] = f(inp["kv_norm_g"][0])
    m["g_qn"] = f(inp["q_norm_nope_g"][0]); m["g_qr"] = f(inp["q_norm_rope_g"][0])
    m["g_kn"] = f(inp["k_norm_nope_g"][0]); m["g_kr"] = f(inp["k_norm_rope_g"][0])
    m["lam_re"] = f(inp["ssm_lam_re"][0]); m["lam_im"] = f(inp["ssm_lam_im"][0]); m["log_dt"] = f(inp["ssm_log_dt"][0])
    m["b_reT"] = f(np.transpose(np.asarray(inp["ssm_b_re"][0]), (0, 2, 1)))
    m["b_imT"] = f(np.transpose(np.asarray(inp["ssm_b_im"][0]), (0, 2, 1)))
    m["c_reT"] = f(np.transpose(np.asarray(inp["ssm_c_re"][0]), (0, 2, 1)))
    m["c_imT"] = f(np.transpose(np.asarray(inp["ssm_c_im"][0]), (0, 2, 1)))
    m["ssm_d"] = f(inp["ssm_d"][0])
    m["ident"] = np.eye(128, dtype=np.float32)
    half = 16
    inv = (10000.0 ** (-np.arange(half, dtype=np.float32) / half)).astype(np.float32)

    def tabs(pos):
        ang = pos.astype(np.float32)[None, :] * inv[:, None]
        c = np.cos(ang).astype(np.float32); s = np.sin(ang).astype(np.float32)
        return np.ascontiguousarray(np.concatenate([c, c], 0)), np.ascontiguousarray(np.concatenate([-s, s], 0))
    m["ropec_p"], m["ropes_p"] = tabs(pos0 + np.arange(SEQ))
    if NPRE:
        m["ropec_pre"], m["ropes_pre"] = tabs(np.arange(SEQ))
    cs, ss = tabs(cfg["PAST"] + np.arange(8))
    m["ropec_s"] = np.ascontiguousarray(np.tile(cs, (1, 16))); m["ropes_s"] = np.ascontiguousarray(np.tile(ss, (1, 16)))
    k = np.arange(128)
    m["tri"] = (k[None, :] >= k[:, None]).astype(np.float32)
    mk = np.zeros((128, 16, 8, 8), np.float32)
    for b in range(16):
        for tk in range(8):
            for tq in range(8):
                if tk <= tq:
                    mk[8 * b + tk, b, :, tq] = 1.0
    m["masks"] = mk.reshape(128, 16 * 64)
    return m


_CACHE = {}


def run(cfg, inp, ncores):
    key = tuple(sorted(cfg.items()))
    if key not in _CACHE:
        _CACHE[key] = build(cfg)
    nc, _ = _CACHE[key]
    in_maps = [_prep(cfg, inp, c, ncores) for c in range(ncores)]
    res = run_bass_kernel_spmd(nc, in_maps, core_ids=list(range(ncores)))
    return res.results


def kernel(**inp):
    B, SEQ = inp["x_prompt"].shape[0], inp["x_prompt"].shape[1]
    DB = inp["x_sample"].shape[0]
    NPOOL = inp["cache_ckv"].shape[1]
    NPG = inp["page_table"].shape[1]
    ncores = 2 * B
    NS = DB // ncores
    HS = SEQ // 2
    cfg = dict(SEQ=HS, NS=NS, NPOOL=NPOOL, NPG=NPG, PAST=NPG * 128, NPRE=HS // NT)
    rs = run(cfg, inp, ncores)
    cat = lambda k: np.stack([np.concatenate([rs[2 * b][k], rs[2 * b + 1][k]], 0) for b in range(B)], 0)
    y_p = cat("y_p")
    y_s = np.concatenate([r["y_s"].reshape(NS, 8, D) for r in rs], 0)
    ckv_p = cat("ckv_p")[None]; kpe_p = cat("kpe_p")[None]
    sre_p = np.stack([rs[2 * b + 1]["sre_p"] for b in range(B)], 0).reshape(1, B, 32, 64)
    sim_p = np.stack([rs[2 * b + 1]["sim_p"] for b in range(B)], 0).reshape(1, B, 32, 64)
    ckv_s = np.concatenate([r["ckv_s"].reshape(NS, 8, 256) for r in rs], 0)[None]
    kpe_s = np.concatenate([r["kpe_s"].reshape(NS, 8, 32) for r in rs], 0)[None]
    sre_s = np.concatenate([r["sre_s"].reshape(NS, 32, 64) for r in rs], 0)[None]
    sim_s = np.concatenate([r["sim_s"].reshape(NS, 32, 64) for r in rs], 0)[None]
    return tuple(np.ascontiguousarray(a, dtype=np.float32) for a in
                 (y_p, y_s, ckv_p, kpe_p, sre_p, sim_p, ckv_s, kpe_s, sre_s, sim_s))
```

```python
import math
from contextlib import ExitStack
import numpy as np
import concourse.bass as bass
import concourse.mybir as mybir
from concourse.bass_utils import run_bass_kernel_spmd

F32 = mybir.dt.float32
BF16 = mybir.dt.bfloat16
I32 = mybir.dt.int32
U32 = mybir.dt.uint32
AF = mybir.ActivationFunctionType
ALU = mybir.AluOpType
AX = mybir.AxisListType

D = 1024
D_IN = 3232
D_FF = 2816
PLE = 256
EPS = 1e-6
ATTN_SCALE = 96 ** -0.5
NT = 256
C_U, C_Q, C_KV, C_KR, C_GA, C_GB = 0, 512, 896, 1152, 1184, 2208


class R:
    __slots__ = ("w", "rd")

    def __init__(self):
        self.w = None
        self.rd = {}


class P:
    ENG = ("pe", "act", "dve", "pool", "sp")

    def __init__(self, nc, es):
        self.nc = nc
        self.es = es
        self.q = {e: [] for e in self.ENG}
        self.sem = {}
        self.cnt = {}
        self.epoch = {}
        self.EPOCH_MAX = 30000
        for e in self.ENG:
            self.epoch[e] = 0
            k = e + "#0"
            self.sem[k] = es.enter_context(nc.semaphore("s_" + e + "_0"))
            self.cnt[k] = 0
        self.dq = {}
        self.NDQ = 8
        for qn in ("sp", "pool"):
            for i in range(self.NDQ):
                k = f"d_{qn}{i}"
                self.sem[k] = es.enter_context(nc.semaphore(k))
                self.cnt[k] = 0
            self.dq[qn] = 0
        self.waited = {}
        self.nins = 0

    def _need(self, e, deps):
        for k, v in deps.items():
            if self.waited.get((e, k), 0) < v:
                self.waited[(e, k)] = v
                sem = self.sem[k]
                self.q[e].append(lambda eng, sem=sem, v=v: eng.wait_ge(sem, v))

    def _deps(self, e, reads, writes):
        d = {}

        def add(kv, raw):
            if kv is None:
                return
            k, v = kv
            if k.split("#")[0] == e and (e == "pe" or not raw):
                return
            if d.get(k, 0) < v:
                d[k] = v

        for r in reads:
            add(r.w, True)
        for r in writes:
            add(r.w, False)
            for k, v in r.rd.items():
                add((k, v), False)
        return d

    def op(self, e, fn, reads=(), writes=()):
        self._need(e, self._deps(e, reads, writes))
        ek = e + "#" + str(self.epoch[e])
        if self.cnt[ek] >= self.EPOCH_MAX:
            self.epoch[e] += 1
            ek = e + "#" + str(self.epoch[e])
            self.sem[ek] = self.es.enter_context(self.nc.semaphore("s_" + e + "_" + str(self.epoch[e])))
            self.cnt[ek] = 0
        self.cnt[ek] += 1
        c = self.cnt[ek]
        sem = self.sem[ek]
        self.q[e].append(lambda eng: fn(eng).then_inc(sem, 1))
        self.nins += 1
        for r in reads:
            if r.rd.get(ek, 0) < c:
                r.rd[ek] = c
        for r in writes:
            r.w = (ek, c)
            r.rd = {}

    def dma(self, qn, fn, reads=(), writes=()):
        i = self.dq[qn]
        self.dq[qn] = (i + 1) % self.NDQ
        k = f"d_{qn}{i}"
        d = self._deps(qn, reads, writes)
        if self.cnt[k] > 0:
            d[k] = max(d.get(k, 0), self.cnt[k])
        self._need(qn, d)
        self.cnt[k] += 16
        c = self.cnt[k]
        sem = self.sem[k]
        self.q[qn].append(lambda eng: fn(eng).then_inc(sem, 16))
        self.nins += 1
        for r in reads:
            r.rd[k] = max(r.rd.get(k, 0), c)
        for r in writes:
            r.w = (k, c)
            r.rd = {}

    def barrier(self):
        d = {k: v for k, v in self.cnt.items() if v > 0}
        for e in self.ENG:
            self._need(e, {k: v for k, v in d.items() if k.split("#")[0] != e})

    def finish(self):
        for qn in ("sp", "pool"):
            d = {}
            for i in range(self.NDQ):
                k = f"d_{qn}{i}"
                if self.cnt[k] > 0:
                    d[k] = self.cnt[k]
            self._need(qn, d)
        print("[kernel] instruction counts:", {k: v for k, v in self.cnt.items() if not k.startswith("d_")}, flush=True)
        block = self.es.enter_context(self.nc.Block())
        q = self.q

        @block.tensor
        def _(eng):
            for f in q["pe"]:
                f(eng)

        @block.scalar
        def _(eng):
            for f in q["act"]:
                f(eng)

        @block.vector
        def _(eng):
            for f in q["dve"]:
                f(eng)

        @block.gpsimd
        def _(eng):
            for f in q["pool"]:
                f(eng)

        @block.sync
        def _(eng):
            for f in q["sp"]:
                f(eng)


class Arena:
    def __init__(self, t, nbytes):
        self.t = t
        self.n = nbytes
        self.off = 0

    def reset(self):
        self.off = 0

    def get(self, shape, dt):
        esz = 2 if dt == BF16 else 4
        nel = 1
        for d in shape[1:]:
            nel *= d
        off = (self.off + 3) // 4 * 4
        self.off = off + nel * esz
        assert self.off <= self.n, ("arena overflow", self.off, self.n)
        ap = self.t[0:shape[0], off // 2:(off + nel * esz) // 2]
        if esz == 4:
            ap = ap.bitcast(dt)
        if len(shape) == 3:
            ap = ap.rearrange("q (a b) -> q a b", b=shape[2])
        elif len(shape) == 4:
            ap = ap.rearrange("q (a b c) -> q a b c", b=shape[2], c=shape[3])
        return ap


class Ring:
    def __init__(self, p, name, n, shape, dt, space="sb", arena=None):
        self.t = []
        self.i = 0
        for j in range(n):
            if arena is not None:
                self.t.append((arena.get(shape, dt), R()))
                continue
            if space == "sb":
                t = p.es.enter_context(p.nc.sbuf_tensor(f"{name}{j}", shape, dt))
            else:
                t = p.es.enter_context(p.nc.psum_tensor(f"{name}{j}", shape, dt))
            self.t.append((t, R()))

    def get(self):
        t = self.t[self.i]
        self.i = (self.i + 1) % len(self.t)
        return t


def build(cfg):
    SEQ, NS, NPOOL, NPG = cfg["SEQ"], cfg["NS"], cfg["NPOOL"], cfg["NPG"]
    NPRE = cfg.get("NPRE", 0)
    NPT = SEQ // NT
    KSEQ = (NPRE + NPT) * NT
    NST = NS // 16
    TS = NS * 8
    nc = bass.Bass("TRN2", target_bir_lowering=False)
    es = ExitStack()
    p = P(nc, es)

    def din(name, shape, dt=F32):
        return nc.dram_tensor(name, list(shape), dt, kind="ExternalInput").ap()

    def dout(name, shape, dt=F32):
        return nc.dram_tensor(name, list(shape), dt, kind="ExternalOutput").ap()

    def dscr(name, shape, dt=BF16):
        return nc.dram_tensor(name, list(shape), dt, kind="Internal").ap()

    def sb(name, shape, dt=F32):
        return es.enter_context(nc.sbuf_tensor("t_" + name, list(shape), dt))

    x_p = din("x_p", [SEQ, D]); p_p = din("p_p", [SEQ, PLE])
    if NPRE:
        x_pre = din("x_pre", [NPRE * NT, D])
        ropec_pre = din("ropec_pre", [32, NPRE * NT]); ropes_pre = din("ropes_pre", [32, NPRE * NT])
    pmask_d = din("pmask", [128, 1])
    x_s = din("x_s", [TS, D]); p_s = din("p_s", [TS, PLE])
    cache_ckv = din("cache_ckv", [NPOOL * 32, 1024]); cache_kpe = din("cache_kpe", [NPOOL * 32, 128])
    st_re = din("st_re", [NS, 2048]); st_im = din("st_im", [NS, 2048])
    ptab = din("ptab", [NS, NPG], I32)
    W = {}
    for nm, shp in (("w_in", [D, D_IN]), ("w_uq", [384, 768]), ("w_uk", [256, 512]), ("w_uv", [256, 512]),
                    ("w_ukT", [512, 256]), ("w_glu", [512, 512]), ("w_branch_a", [512, D]),
                    ("w_branch_b", [512, D]), ("w_out", [D, D]), ("w_gate", [D, D_FF]), ("w_up", [D, D_FF]),
                    ("w_down", [D_FF, D]), ("w_ple_gate", [D, D]), ("w_ple", [PLE, D])):
        W[nm] = (din(nm, shp), dscr(nm + "_b", shp), R())
    g_attn = din("g_attn", [D]); g_ffn = din("g_ffn", [D]); g_ple = din("g_ple", [D])
    g_q = din("g_q", [384]); g_kv = din("g_kv", [256])
    g_qn = din("g_qn", [64]); g_qr = din("g_qr", [16]); g_kn = din("g_kn", [64]); g_kr = din("g_kr", [16])
    lam_re = din("lam_re", [32, 64]); lam_im = din("lam_im", [32, 64]); log_dt = din("log_dt", [32])
    b_reT = din("b_reT", [32, 16, 64]); b_imT = din("b_imT", [32, 16, 64])
    c_reT = din("c_reT", [32, 64, 16]); c_imT = din("c_imT", [32, 64, 16])
    ssm_d = din("ssm_d", [512])
    ident_d = din("ident", [128, 128])
    ropec_p = din("ropec_p", [32, SEQ]); ropes_p = din("ropes_p", [32, SEQ])
    ropec_s = din("ropec_s", [32, 128]); ropes_s = din("ropes_s", [32, 128])
    tri_d = din("tri", [128, 128])
    masks_d = din("masks", [128, 16 * 64])

    y_p = dout("y_p", [SEQ, D]); y_s = dout("y_s", [TS, D])
    ckv_p = dout("ckv_p", [SEQ, 256]); kpe_p = dout("kpe_p", [SEQ, 32])
    sre_p = dout("sre_p", [1, 2048]); sim_p = dout("sim_p", [1, 2048])
    ckv_s = dout("ckv_s", [TS, 256]); kpe_s = dout("kpe_s", [TS, 32])
    sre_s = dout("sre_s", [NS, 2048]); sim_s = dout("sim_s", [NS, 2048])

    Kscr = dscr("Kscr", [8, 96, KSEQ]); Kscr_r = R()
    Vscr = dscr("Vscr", [KSEQ // 128, 128, 8 * 65]); Vscr_r = R()

    PS = Ring(p, "ps", 5, [128, 512], F32, "ps")
    spb_t = es.enter_context(nc.psum_tensor("spb", [128, 512], F32))
    SPB = Ring(p, "spb", 0, [128, 64], F32)
    SPB.t = [(spb_t[:, 64 * k:64 * k + 64], R()) for k in range(8)]
    SPBH = R()
    PSO = Ring(p, "pso", 2, [128, 512], F32, "ps")
    WR = Ring(p, "wr", 2, [128, 3072], BF16)

    def act(out, in_, func, reads, writes, scale=None, bias=None, accum=None):
        kw = {}
        if scale is not None:
            kw["scale"] = scale
        if bias is not None:
            kw["bias"] = bias
        if accum is not None:
            kw["accum_out"] = accum
        p.op("act", lambda e: e.activation(out=out, in_=in_, func=func, **kw), reads, writes)

    def tt(eng, out, in0, in1, op, reads, writes):
        p.op(eng, lambda e: e.tensor_tensor(out=out, in0=in0, in1=in1, op=op), reads, writes)

    def ts(eng, out, in0, s1, s2, op0, op1, reads, writes):
        if op1 is None:
            p.op(eng, lambda e: e.tensor_scalar(out=out, in0=in0, scalar1=s1, scalar2=None, op0=op0), reads, writes)
        else:
            p.op(eng, lambda e: e.tensor_scalar(out=out, in0=in0, scalar1=s1, scalar2=s2, op0=op0, op1=op1),
                 reads, writes)

    def stt(out, in0, scalar, in1, op0, op1, reads, writes):
        p.op("dve", lambda e: e.scalar_tensor_tensor(out=out, in0=in0, scalar=scalar, in1=in1, op0=op0, op1=op1),
             reads, writes)

    def cp(eng, out, in_, reads, writes):
        if eng == "act":
            p.op("act", lambda e: e.activation(out=out, in_=in_, func=AF.Copy), reads, writes)
        else:
            p.op(eng, lambda e: e.tensor_copy(out=out, in_=in_), reads, writes)

    def mm(out, lhsT, rhs, start, stop, reads, writes):
        p.op("pe", lambda e: e.matmul(out, lhsT, rhs, start=start, stop=stop), reads, writes)

    def tr(out, in_, ident, reads, writes):
        p.op("pe", lambda e: e.transpose(out, in_, ident), reads, writes)

    def ld(out, in_, reads, writes, q="sp"):
        p.dma(q, lambda e: e.dma_start(out=out, in_=in_, allow_slow_non_contiguous=True), reads, writes)

    def memset(eng, ap, val, writes):
        p.op(eng, lambda e: e.memset(ap, val), (), writes)

    ARENA_BYTES = 42 * 1024
    arena_t = sb("arena", [128, ARENA_BYTES // 2], BF16)
    arena = Arena(arena_t, ARENA_BYTES)

    ident = sb("ident", [128, 128]); ident_r = R()
    ld(ident[:], ident_d, (), (ident_r,))
    identb = sb("identb", [128, 128], BF16); identb_r = R()
    cp("dve", identb[:], ident[:], (ident_r,), (identb_r,))
    ones = sb("ones", [128, 128], BF16); ones_r = R()
    memset("dve", ones[:], 1.0, (ones_r,))
    epsT = sb("epsT", [128, 1]); eps_r = R()
    memset("dve", epsT[:], EPS, (eps_r,))
    pmask = sb("pmask", [128, 1]); pmask_r = R()
    ld(pmask[:], pmask_d, (), (pmask_r,))
    tri = sb("tri", [128, 128]); tri_r = R()
    ld(tri[:], tri_d, (), (tri_r,))
    masks = sb("masks", [128, 16 * 64], BF16); masks_r = R()
    ld(masks[:], masks_d, (), (masks_r,), q="pool")

    def gain_tile(name, src, n, q=128):
        kc = (n + q - 1) // q
        t = sb(name, [128, kc]); r = R()
        with nc.allow_non_contiguous_dma(reason="tiny gain load"):
            full = n // q
            ld(t[0:q, 0:full], src[0:full * q].rearrange("(c q) -> q c", q=q), (), (r,))
            if n > full * q:
                ld(t[0:n - full * q, full:full + 1], src[full * q:n].rearrange("(q c) -> q c", c=1), (r,), (r,))
        return t, r

    gA, gA_r = gain_tile("gA", g_attn, D)
    gF, gF_r = gain_tile("gF", g_ffn, D)
    gP, gP_r = gain_tile("gP", g_ple, D)
    gQ, gQ_r = gain_tile("gQ", g_q, 384)
    gKV, gKV_r = gain_tile("gKV", g_kv, 256)
    dS, dS_r = gain_tile("dS", ssm_d, 512, q=96)
    gqh = sb("gqh", [96, 1]); gqh_r = R()
    gkh = sb("gkh", [96, 1]); gkh_r = R()
    gqk = sb("gqk", [96, 1]); gqk_r = R()
    with nc.allow_non_contiguous_dma(reason="tiny gain load"):
        for t, r, gn, gr in ((gqh, gqh_r, g_qn, g_qr), (gkh, gkh_r, g_kn, g_kr)):
            ld(t[0:64, :], gn.rearrange("(q c) -> q c", c=1), (), (r,))
            ld(t[64:80, :], gr.rearrange("(q c) -> q c", c=1), (r,), (r,))
            ld(t[80:96, :], gr.rearrange("(q c) -> q c", c=1), (r,), (r,))

    for nm, (src, dst, r) in W.items():
        rows = src.shape[0]
        step = 512
        for r0 in range(0, rows, step):
            r1 = min(rows, r0 + step)
            p.dma("pool", lambda e, dst=dst, src=src, r0=r0, r1=r1: e.dma_start(
                out=dst[r0:r1, :], in_=src[r0:r1, :], max_dma_last_dim=4096), (), (r,))

    xtok = sb("xtok", [128, 2, D]); xtok_r = [R(), R()]
    xT = sb("xT", [128, 8, NT]); xT_r = [R() for _ in range(8)]
    xnT = sb("xnT", [128, 8, NT], BF16); xnT_r = [R() for _ in range(8)]
    SQ = Ring(p, "sq", 2, [128, NT], BF16)
    rstd = sb("rstd", [128, NT]); rstd_r = R()
    lnt = sb("lnt", [128, NT]); lnt_r = R()

    def norm_fm(src_aps, src_rs, rows, dim, gain, gain_r, out_aps, out_rs, N, out2=None):
        kc = len(src_aps)
        pt, pr = PS.get()
        for c in range(kc):
            sq, sqr = SQ.get()
            act(sq[0:rows, 0:N], src_aps[c], AF.Square, (src_rs[c],), (sqr,))
            mm(pt[0:rows, 0:N], ones[0:rows, 0:rows], sq[0:rows, 0:N], c == 0, c == kc - 1,
               (sqr, ones_r), (pr,))
        act(lnt[0:rows, 0:N], pt[0:rows, 0:N], AF.Ln, (pr, eps_r), (lnt_r,), scale=1.0 / dim, bias=epsT[0:rows, :])
        act(rstd[0:rows, 0:N], lnt[0:rows, 0:N], AF.Exp, (lnt_r,), (rstd_r,), scale=-0.5)
        for c in range(kc):
            ws = (out_rs[c],) if out2 is None else (out_rs[c],)
            stt(out_aps[c], src_aps[c], gain[0:rows, c:c + 1], rstd[0:rows, 0:N], ALU.mult, ALU.mult,
                (src_rs[c], gain_r, rstd_r), ws)
            if out2 is not None:
                cp("pool", out2[0][c], out_aps[c], (out_rs[c],), (out2[1][c],))

    def linear_fm(wname, col0, ncols, rhs_aps, rhs_rs, N, epi, krows=128, mchunk=128):
        src, wb, wr_ = W[wname]
        kc = len(rhs_aps)
        g = max(mchunk, min(1024, 3072 // kc) // mchunk * mchunk)
        c = 0
        while c < ncols:
            gc = min(g, ncols - c)
            wt, wtr = WR.get()
            wv = wt[:, 0:kc * gc].rearrange("q (k n) -> q k n", k=kc)
            if kc * krows == src.shape[0] and krows == 128:
                ld(wv, wb[:, col0 + c:col0 + c + gc].rearrange("(k q) n -> q k n", q=128), (wr_,), (wtr,))
            else:
                for k in range(kc):
                    rws = min(krows, src.shape[0] - k * krows)
                    ld(wv[0:rws, k, :], wb[k * krows:k * krows + rws, col0 + c:col0 + c + gc], (wr_,), (wtr,))
            for m0 in range(0, gc, mchunk):
                mr = min(mchunk, gc - m0)
                pt, pr = PS.get()
                for k in range(kc):
                    rws = rhs_aps[k].shape[0]
                    mm(pt[0:mr, 0:N], wv[0:rws, k, m0:m0 + mr], rhs_aps[k], k == 0, k == kc - 1,
                       (wtr, rhs_rs[k]), (pr,))
                epi((c + m0) // mchunk, pt, pr, mr)
            c += gc

    uTb = sb("uTb", [96, 6, NT], BF16); uTb_r = [R() for _ in range(6)]
    qlT = sb("qlT", [128, 3, NT]); qlT_r = [R() for _ in range(3)]
    qcT = sb("qcT", [128, 3, NT], BF16); qcT_r = [R() for _ in range(3)]
    kvT = sb("kvT", [128, 2, NT]); kvT_r = [R() for _ in range(2)]
    ckvT = sb("ckvT", [128, 2, NT]); ckvT_r = [R() for _ in range(2)]
    ckvTb = sb("ckvTb", [128, 2, NT], BF16); ckvTb_r = [R() for _ in range(2)]
    sga = sb("sga", [128, 8, NT], BF16); sga_r = [R() for _ in range(8)]
    sgb = sb("sgb", [128, 8, NT], BF16); sgb_r = [R() for _ in range(8)]
    kr1 = sb("kr1", [128, NT]); kr1_r = R()
    kr2 = sb("kr2", [128, NT]); kr2_r = R()
    kpeT = sb("kpeT", [128, NT]); kpeT_r = R()
    rc = sb("rc", [128, NT]); rc_r = R()
    rs_ = sb("rs", [128, NT]); rs_r = R()
    zT = sb("zT", [96, 6, NT]); zT_r = [R() for _ in range(6)]
    zTb = sb("zTb", [96, 6, NT], BF16); zTb_r = [R() for _ in range(6)]
    aoT = sb("aoT", [96, 6, NT], BF16); aoT_r = [R() for _ in range(6)]
    boT = sb("boT", [128, 4, NT], BF16); boT_r = [R() for _ in range(4)]
    mgTb = sb("mgTb", [128, 8, NT], BF16); mgTb_r = [R() for _ in range(8)]
    hT = sb("hT", [128, 22, NT], BF16); hT_r = [R() for _ in range(22)]
    pT = sb("pT", [128, 2, NT], BF16); pT_r = [R() for _ in range(2)]
    ptok = sb("ptok", [128, 2, PLE]); ptok_r = [R(), R()]
    gtT = sb("gtT", [128, 8, NT], BF16); gtT_r = [R() for _ in range(8)]
    tmpA = Ring(p, "tmpA", 2, [128, NT], F32)
    QH = Ring(p, "qh", 1, [96, NT], F32)
    qhb = sb("qhb", [96, 8, NT], BF16); qhb_r = [R() for _ in range(8)]
    qs1, qs1_r, qs2, qs2_r = kr1, kr1_r, kr2, kr2_r
    kh = sb("kh", [96, NT]); kh_r = R()
    khb = sb("khb", [96, NT], BF16); khb_r = R()
    otok = xtok[:, :, 0:256]; otok_r = xtok_r
    ktok = sb("ktok", [128, 2, 32]); ktok_r = [R(), R()]
    vtok = sb("vtok", [128, 2, 8, 65], BF16); vtok_r = [R(), R()]
    memset("pool", vtok[:], 1.0, (vtok_r[0], vtok_r[1]))

    NU = 16
    lr = sb("lr", [128, NU]); li = sb("li", [128, NU]); dtt = sb("dtt", [128, NU])
    ssm_r = R()
    with nc.allow_non_contiguous_dma(reason="tiny ssm param loads"):
        ld(lr[:], lam_re.rearrange("(u g) n -> g n u", g=2).rearrange("g n u -> (g n) u"), (), (ssm_r,))
        ld(li[:], lam_im.rearrange("(u g) n -> g n u", g=2).rearrange("g n u -> (g n) u"), (ssm_r,), (ssm_r,))
        ldt = log_dt.rearrange("(u g) -> g u", g=2)
        ld(dtt[0:64, :], ldt[0:1, :].partition_broadcast(64) if False else ldt[0:1, :].broadcast_to([64, NU]),
           (ssm_r,), (ssm_r,))
        ld(dtt[64:128, :], ldt[1:2, :].broadcast_to([64, NU]), (ssm_r,), (ssm_r,))
    sm = {}
    for nm in ("dt", "rho", "ang", "a1", "a2", "cs", "sn", "fre", "fim", "den", "t1", "t2", "nr", "ni", "fir", "fii"):
        sm[nm] = sb("sm_" + nm, [128, NU])
    smi = sb("smi", [128, NU], I32)
    S = (ssm_r,)

    def sop(eng, f):
        p.op(eng, f, S, S)

    act(sm["dt"][:], dtt[:], AF.Exp, S, S)
    tt("dve", sm["a1"][:], lr[:], sm["dt"][:], ALU.mult, S, S)
    act(sm["rho"][:], sm["a1"][:], AF.Exp, S, S)
    tt("dve", sm["ang"][:], li[:], sm["dt"][:], ALU.mult, S, S)

    def sin_of(out, ang_ap, shift):
        ts("dve", sm["t1"][:], ang_ap, shift, 1.0 / (2 * math.pi), ALU.add, ALU.mult, S, S)
        cp("dve", smi[:], sm["t1"][:], S, S)
        cp("dve", sm["t2"][:], smi[:], S, S)
        tt("dve", sm["t1"][:], sm["t1"][:], sm["t2"][:], ALU.subtract, S, S)
        ts("dve", sm["t2"][:], sm["t1"][:], 0.5, None, ALU.is_gt, None, S, S)
        tt("dve", sm["t1"][:], sm["t1"][:], sm["t2"][:], ALU.subtract, S, S)
        ts("dve", sm["t2"][:], sm["t1"][:], -0.5, None, ALU.is_lt, None, S, S)
        tt("dve", sm["t1"][:], sm["t1"][:], sm["t2"][:], ALU.add, S, S)
        act(out, sm["t1"][:], AF.Sin, S, S, scale=2 * math.pi)

    sin_of(sm["sn"][:], sm["ang"][:], 0.0)
    sin_of(sm["cs"][:], sm["ang"][:], math.pi / 2)
    tt("dve", sm["nr"][:], sm["rho"][:], sm["cs"][:], ALU.mult, S, S)
    ts("dve", sm["nr"][:], sm["nr"][:], -1.0, None, ALU.add, None, S, S)
    tt("dve", sm["ni"][:], sm["rho"][:], sm["sn"][:], ALU.mult, S, S)
    tt("dve", sm["t1"][:], lr[:], lr[:], ALU.mult, S, S)
    tt("dve", sm["t2"][:], li[:], li[:], ALU.mult, S, S)
    tt("dve", sm["den"][:], sm["t1"][:], sm["t2"][:], ALU.add, S, S)
    p.op("dve", lambda e: e.reciprocal(out=sm["den"][:], in_=sm["den"][:]), S, S)
    tt("dve", sm["t1"][:], sm["nr"][:], lr[:], ALU.mult, S, S)
    tt("dve", sm["t2"][:], sm["ni"][:], li[:], ALU.mult, S, S)
    tt("dve", sm["t1"][:], sm["t1"][:], sm["t2"][:], ALU.add, S, S)
    tt("dve", sm["fre"][:], sm["t1"][:], sm["den"][:], ALU.mult, S, S)
    tt("dve", sm["t1"][:], sm["ni"][:], lr[:], ALU.mult, S, S)
    tt("dve", sm["t2"][:], sm["nr"][:], li[:], ALU.mult, S, S)
    tt("dve", sm["t1"][:], sm["t1"][:], sm["t2"][:], ALU.subtract, S, S)
    tt("dve", sm["fim"][:], sm["t1"][:], sm["den"][:], ALU.mult, S, S)
    tt("dve", sm["t1"][:], sm["fre"][:], sm["fre"][:], ALU.mult, S, S)
    tt("dve", sm["t2"][:], sm["fim"][:], sm["fim"][:], ALU.mult, S, S)
    tt("dve", sm["t1"][:], sm["t1"][:], sm["t2"][:], ALU.add, S, S)
    p.op("dve", lambda e: e.reciprocal(out=sm["t1"][:], in_=sm["t1"][:]), S, S)
    tt("dve", sm["fir"][:], sm["fre"][:], sm["t1"][:], ALU.mult, S, S)
    tt("dve", sm["fii"][:], sm["fim"][:], sm["t1"][:], ALU.mult, S, S)
    ts("dve", sm["fii"][:], sm["fii"][:], -1.0, None, ALU.mult, None, S, S)

    Tc = sb("Tc", [128, NU, NT]); Ts = sb("Ts", [128, NU, NT]); tab_r = R()
    TB = (tab_r,)
    cp("dve", Tc[:, :, 0:1], sm["cs"][:].rearrange("q (u o) -> q u o", o=1), S, TB)
    cp("dve", Ts[:, :, 0:1], sm["sn"][:].rearrange("q (u o) -> q u o", o=1), S, TB)
    xflat = xT[:].rearrange("q c n -> q (c n)")
    HU = NU // 2
    tw = [xflat[:, i * HU * (NT // 2):(i + 1) * HU * (NT // 2)].rearrange("q (u n) -> q u n", u=HU) for i in range(2)]
    TB = (tab_r,) + tuple(xT_r)
    n = 1
    while n < NT:
        for u0 in (0, HU):
            us = slice(u0, u0 + HU)
            mc = Tc[:, us, n - 1:n].broadcast_to([128, HU, n])
            ms = Ts[:, us, n - 1:n].broadcast_to([128, HU, n])
            a, b = tw[0][:, :, 0:n], tw[1][:, :, 0:n]
            tt("dve", a, Tc[:, us, 0:n], mc, ALU.mult, TB, TB)
            tt("dve", b, Ts[:, us, 0:n], ms, ALU.mult, TB, TB)
            tt("dve", Tc[:, us, n:2 * n], a, b, ALU.subtract, TB, TB)
            tt("dve", a, Tc[:, us, 0:n], ms, ALU.mult, TB, TB)
            tt("dve", b, Ts[:, us, 0:n], mc, ALU.mult, TB, TB)
            tt("dve", Ts[:, us, n:2 * n], a, b, ALU.add, TB, TB)
        n *= 2
    TB = (tab_r,)
    decs = hT[:].rearrange("q c n -> q (c n)")[:, 0:2 * NU * 128].bitcast(F32).rearrange("q (u n) -> q u n", u=NU)
    decs_rs = tuple(hT_r)

    def build_decs():
        cp("dve", decs.rearrange("q u (s t) -> q u s t", t=8),
           sm["rho"][:].rearrange("q (u s t) -> q u s t", s=1, t=1).broadcast_to([128, NU, 16, 8]), S, decs_rs)
        memset("dve", decs.rearrange("q u (s t) -> q u s t", t=8)[:, :, :, 0:1], 0.0, decs_rs)

    Bwb = sb("Bwb", [96, 6, 2, 128], BF16); bw_r = R()
    memset("pool", Bwb[:], 0.0, (bw_r,))
    for u in range(NU):
        for g2 in range(2):
            for ri, src in enumerate((b_reT, b_imT)):
                ld(Bwb[32 * (u % 3) + 16 * g2:32 * (u % 3) + 16 * g2 + 16, u // 3, ri, 64 * g2:64 * g2 + 64],
                   src[2 * u + g2], (bw_r,), (bw_r,), q="pool")
    hflat = hT[:].rearrange("q c n -> q (c n)")
    def hview(i):
        return hflat[:, i * 512:(i + 1) * 512].bitcast(F32).rearrange("q (u c) -> q u c", u=NU)
    Cr = hview(0); Ci = hview(1); cw_r = R()
    with nc.allow_non_contiguous_dma(reason="ssm C load"):
        ld(Cr[:], c_reT.rearrange("(u g) n c -> g n u c", g=2).rearrange("g n u c -> (g n) u c"), (), (cw_r,))
        ld(Ci[:], c_imT.rearrange("(u g) n c -> g n u c", g=2).rearrange("g n u c -> (g n) u c"), (cw_r,), (cw_r,))
    Cw = sb("Cw", [128, NU, 2, 32], BF16)
    memset("pool", Cw[:], 0.0, (cw_r,))
    ct = [hview(2 + i) for i in range(3)]
    CS = (cw_r, ssm_r)
    frb = sm["fre"][:].rearrange("q (u o) -> q u o", o=1).broadcast_to([128, NU, 16])
    fib = sm["fim"][:].rearrange("q (u o) -> q u o", o=1).broadcast_to([128, NU, 16])
    tt("dve", ct[0][:], Cr[:], frb, ALU.mult, CS, (cw_r,))
    tt("dve", ct[1][:], Ci[:], fib, ALU.mult, CS, (cw_r,))
    tt("dve", ct[2][:], ct[0][:], ct[1][:], ALU.subtract, CS, (cw_r,))
    for g2 in range(2):
        cp("dve", Cw[64 * g2:64 * g2 + 64, :, 0, 16 * g2:16 * g2 + 16], ct[2][64 * g2:64 * g2 + 64, :, :],
           (cw_r,), (cw_r,))
    tt("dve", ct[0][:], Cr[:], fib, ALU.mult, CS, (cw_r,))
    tt("dve", ct[1][:], Ci[:], frb, ALU.mult, CS, (cw_r,))
    tt("dve", ct[2][:], ct[0][:], ct[1][:], ALU.add, CS, (cw_r,))
    for g2 in range(2):
        ts("dve", Cw[64 * g2:64 * g2 + 64, :, 1, 16 * g2:16 * g2 + 16], ct[2][64 * g2:64 * g2 + 64, :, :],
           -1.0, None, ALU.mult, None, (cw_r,), (cw_r,))

    carry = sb("carry", [128, NU, 2]); carry_rs = [R() for _ in range(NU)]
    memset("dve", carry[:], 0.0, tuple(carry_rs))
    SW = {nm: Ring(p, "sw_" + nm, 3 if nm in "ac" else 2, [128, NT], F32) for nm in ("a", "b", "c", "d", "gr", "gi")}
    HB = Ring(p, "hb", 2, [128, NT], BF16)
    h0_r = R(); fs_rs = [R() for _ in range(NU)]
    SMP = {}

    IL = [None, 1.0, 0.0]

    def tick(w=1.0):
        if IL[0] is not None:
            IL[2] += w * IL[1]
            while IL[2] >= 1.0 and IL[0] is not None:
                IL[2] -= 1.0
                if next(IL[0], None) is None:
                    IL[0] = None

    def drain():
        while IL[0] is not None:
            tick(1.0)
        IL[2] = 0.0

    def ssm_tile(N, sample, lite=False):
        for _ in ssm_gen(N, sample, lite):
            pass

    def ssm_gen(N, sample, lite=False):
        yacc = {}
        stt_ = {}
        st3 = {}

        def views(u):
            if sample:
                tcs = Tc[:, u, 0:8].rearrange("q (s t) -> q s t", s=1).broadcast_to([128, 16, 8])
                tss = Ts[:, u, 0:8].rearrange("q (s t) -> q s t", s=1).broadcast_to([128, 16, 8])
                v3 = lambda ap: ap.rearrange("q (s t) -> q s t", t=8)
            else:
                tcs, tss = Tc[:, u, 0:N], Ts[:, u, 0:N]
                v3 = lambda ap: ap
            return tcs, tss, v3

        def s1(u):
            ch, po = u // 3, 32 * (u % 3)
            tcs, tss, v3 = views(u)
            pre, prr = PS.get()
            mm(pre[:, 0:N], Bwb[po:po + 32, ch, 0, :], uTb[po:po + 32, ch, 0:N], True, True,
               (bw_r, uTb_r[ch]), (prr,))
            pim, pir = PS.get()
            mm(pim[:, 0:N], Bwb[po:po + 32, ch, 1, :], uTb[po:po + 32, ch, 0:N], True, True,
               (bw_r, uTb_r[ch]), (pir,))
            (a, ar), (b, br), (c, cr), (d, dr) = SW["a"].get(), SW["b"].get(), SW["c"].get(), SW["d"].get()
            tt("dve", v3(a[:, 0:N]), v3(pre[:, 0:N]), tcs, ALU.mult, (prr, tab_r), (ar,))
            tt("dve", v3(b[:, 0:N]), v3(pim[:, 0:N]), tss, ALU.mult, (pir, tab_r), (br,))
            tt("dve", v3(c[:, 0:N]), v3(pim[:, 0:N]), tcs, ALU.mult, (pir, tab_r), (cr,))
            tt("dve", v3(d[:, 0:N]), v3(pre[:, 0:N]), tss, ALU.mult, (prr, tab_r), (dr,))
            tt("dve", a[:, 0:N], a[:, 0:N], b[:, 0:N], ALU.add, (ar, br), (ar,))
            tt("dve", c[:, 0:N], c[:, 0:N], d[:, 0:N], ALU.subtract, (cr, dr), (cr,))
            stt_[u] = (a, ar, b, br, c, cr, d, dr)

        def s2(u):
            ch, po = u // 3, 32 * (u % 3)
            tcs, tss, v3 = views(u)
            a, ar, b, br, c, cr, d, dr = stt_[u]
            (gr, grr), (gi, gir) = SW["gr"].get(), SW["gi"].get()
            st3[u] = (gr, grr, gi, gir)
            if sample:
                rho_u = sm["rho"][:, u:u + 1]
                a0 = a[:, 0:N].rearrange("q (s t) -> q s t", t=8)[:, :, 0]
                c0 = c[:, 0:N].rearrange("q (s t) -> q s t", t=8)[:, :, 0]
                stt(a0, SMP["hpr"][:, u, :], rho_u, a0, ALU.mult, ALU.add, (h0_r, ssm_r, ar), (ar,))
                stt(c0, SMP["hpi"][:, u, :], rho_u, c0, ALU.mult, ALU.add, (h0_r, ssm_r, cr), (cr,))
                dec = decs[:, u, 0:N]
                dec_rs = decs_rs
                i_re, i_im = 0.0, 0.0
            else:
                dec = sm["rho"][:, u:u + 1].broadcast_to([128, N])
                dec_rs = ()
                i_re, i_im = carry[:, u, 0:1], carry[:, u, 1:2]
            cu = carry_rs[u]
            p.op("dve", lambda e, gr=gr, dec=dec, a=a, i_re=i_re: e.tensor_tensor_scan(
                out=gr[:, 0:N], data0=dec, data1=a[:, 0:N], initial=i_re, op0=ALU.mult, op1=ALU.add),
                (ar, tab_r, ssm_r, cu) + dec_rs, (grr,))
            p.op("dve", lambda e, gi=gi, dec=dec, c=c, i_im=i_im: e.tensor_tensor_scan(
                out=gi[:, 0:N], data0=dec, data1=c[:, 0:N], initial=i_im, op0=ALU.mult, op1=ALU.add),
                (cr, tab_r, ssm_r, cu) + dec_rs, (gir,))
            if lite:
                L = slice(N - 1, N)
                tt("pool", a[:, L], gr[:, L], Tc[:, u, L], ALU.mult, (grr, tab_r), (ar,))
                tt("pool", b[:, L], gi[:, L], Ts[:, u, L], ALU.mult, (gir, tab_r), (br,))
                tt("pool", c[:, L], gr[:, L], Ts[:, u, L], ALU.mult, (grr, tab_r), (cr,))
                tt("pool", d[:, L], gi[:, L], Tc[:, u, L], ALU.mult, (gir, tab_r), (dr,))
                tt("dve", carry[:, u, 0:1], a[:, L], b[:, L], ALU.subtract, (ar, br, grr, gir), (cu,))
                tt("dve", carry[:, u, 1:2], c[:, L], d[:, L], ALU.add, (cr, dr, grr, gir), (cu,))
                return
            tt("pool", v3(a[:, 0:N]), v3(gr[:, 0:N]), tcs, ALU.mult, (grr, tab_r), (ar,))
            tt("pool", v3(b[:, 0:N]), v3(gi[:, 0:N]), tss, ALU.mult, (gir, tab_r), (br,))
            tt("pool", v3(c[:, 0:N]), v3(gr[:, 0:N]), tss, ALU.mult, (grr, tab_r), (cr,))
            tt("pool", v3(d[:, 0:N]), v3(gi[:, 0:N]), tcs, ALU.mult, (gir, tab_r), (dr,))

        def s3(u):
            if lite:
                return
            ch, po = u // 3, 32 * (u % 3)
            a, ar, b, br, c, cr, d, dr = stt_.pop(u)
            gr, grr, gi, gir = st3.pop(u)
            cu = carry_rs[u]
            (hr, hrr), (hi, hir) = HB.get(), HB.get()
            tt("dve", hr[:, 0:N], a[:, 0:N], b[:, 0:N], ALU.subtract, (ar, br), (hrr,))
            tt("dve", hi[:, 0:N], c[:, 0:N], d[:, 0:N], ALU.add, (cr, dr), (hir,))
            if sample:
                l3 = lambda ap: ap[:, 0:N].rearrange("q (s t) -> q s t", t=8)[:, :, 7]
                tt("dve", SMP["fsr"][:, u, :], l3(a), l3(b), ALU.subtract, (ar, br), (fs_rs[u],))
                tt("dve", SMP["fsi"][:, u, :], l3(c), l3(d), ALU.add, (cr, dr), (fs_rs[u],))
            else:
                tt("dve", carry[:, u, 0:1], a[:, N - 1:N], b[:, N - 1:N], ALU.subtract, (ar, br, grr, gir), (cu,))
                tt("dve", carry[:, u, 1:2], c[:, N - 1:N], d[:, N - 1:N], ALU.add, (cr, dr, grr, gir), (cu,))
            if ch not in yacc:
                if sample:
                    yacc[ch] = PSO.get()
                else:
                    yacc[ch] = (spb_t[:, 0:256], SPBH)
            yp, ypr = yacc[ch]
            mm(yp[po:po + 32, 0:N], Cw[:, u, 0, :], hr[:, 0:N], True, False, (cw_r, hrr), (ypr,))
            mm(yp[po:po + 32, 0:N], Cw[:, u, 1, :], hi[:, 0:N], False, True, (cw_r, hir), (ypr,))
            if u % 3 == 2 or u == NU - 1:
                zr = 96 if ch < 5 else 32
                ta, tar = tmpA.get()
                stt(zT[0:zr, ch, 0:N], uTb[0:zr, ch, 0:N], dS[0:zr, ch:ch + 1], yp[0:zr, 0:N], ALU.mult, ALU.add,
                    (uTb_r[ch], dS_r, ypr), (zT_r[ch],))
                act(ta[0:zr, 0:N], zT[0:zr, ch, 0:N], AF.Square, (zT_r[ch],), (tar,))
                ts("dve", ta[0:zr, 0:N], ta[0:zr, 0:N], 0.044715, 1.0, ALU.mult, ALU.add, (tar,), (tar,))
                tt("dve", ta[0:zr, 0:N], ta[0:zr, 0:N], zT[0:zr, ch, 0:N], ALU.mult, (tar, zT_r[ch]), (tar,))
                act(ta[0:zr, 0:N], ta[0:zr, 0:N], AF.Sigmoid, (tar,), (tar,), scale=1.5957691216057308)
                tt("dve", zT[0:zr, ch, 0:N], zT[0:zr, ch, 0:N], ta[0:zr, 0:N], ALU.mult, (tar, zT_r[ch]), (zT_r[ch],))
                cp("pool", zTb[0:zr, ch, 0:N], zT[0:zr, ch, 0:N], (zT_r[ch],), (zTb_r[ch],))

        s1(0)
        for u in range(NU + 1):
            if u >= 1:
                s3(u - 1)
            if u < NU:
                s2(u)
            if u + 1 < NU:
                s1(u + 1)
            yield u

    def front(xsrc, psrc, t0, N, rcd, rsd, lite=False):
        nsub = N // 128
        for s in range(nsub):
            ld(xtok[:, s, :], xsrc[t0 + 128 * s:t0 + 128 * s + 128, :], (), (xtok_r[s],))
            if not lite:
                ld(ptok[:, s, :], psrc[t0 + 128 * s:t0 + 128 * s + 128, :], (), (ptok_r[s],))
        ld(rc[64:96, 0:N], rcd, (), (rc_r,))
        ld(rs_[64:96, 0:N], rsd, (), (rs_r,))
        for s in range(nsub):
            for c in range(8):
                pt, pr = PS.get()
                tr(pt[:, 0:128], xtok[:, s, 128 * c:128 * c + 128], ident[:], (xtok_r[s], ident_r), (pr,))
                cp("act" if c % 2 else "dve", xT[:, c, 128 * s:128 * s + 128], pt[:, 0:128], (pr,), (xT_r[c],))
            for c in range(0 if lite else 2):
                pt, pr = PS.get()
                tr(pt[:, 0:128], ptok[:, s, 128 * c:128 * c + 128], ident[:], (ptok_r[s], ident_r), (pr,))
                cp("act", pT[:, c, 128 * s:128 * s + 128], pt[:, 0:128], (pr,), (pT_r[c],))
        norm_fm([xT[:, c, 0:N] for c in range(8)], xT_r, 128, D, gA, gA_r,
                [xnT[:, c, 0:N] for c in range(8)], xnT_r, N)
        rhs = [xnT[:, c, 0:N] for c in range(8)]

        def epi_u(m, pt, pr, mr):
            cp("dve", uTb[0:mr, m, 0:N], pt[0:mr, 0:N], (pr,), (uTb_r[m],))
        linear_fm("w_in", C_U, 512, rhs, xnT_r, N, epi_u, mchunk=96)

        def epi_q(m, pt, pr, mr):
            cp("act", qlT[:, m, 0:N], pt[:, 0:N], (pr,), (qlT_r[m],))
        if not lite:
            linear_fm("w_in", C_Q, 384, rhs, xnT_r, N, epi_q)

        def epi_kv(m, pt, pr, mr):
            cp("act", kvT[:, m, 0:N], pt[:, 0:N], (pr,), (kvT_r[m],))
        linear_fm("w_in", C_KV, 256, rhs, xnT_r, N, epi_kv)

        def epi_ga(m, pt, pr, mr):
            act(sga[:, m, 0:N], pt[:, 0:N], AF.Sigmoid, (pr,), (sga_r[m],))
        if not lite:
            linear_fm("w_in", C_GA, 1024, rhs, xnT_r, N, epi_ga)

        def epi_gb(m, pt, pr, mr):
            act(sgb[:, m, 0:N], pt[:, 0:N], AF.Sigmoid, (pr,), (sgb_r[m],))
        if not lite:
            linear_fm("w_in", C_GB, 1024, rhs, xnT_r, N, epi_gb)
        wt, wtr = WR.get()
        wv = wt[:, 0:8 * 64].rearrange("q (k n) -> q k n", k=8)
        wb = W["w_in"][1]; wr_ = W["w_in"][2]
        ld(wv[:, :, 0:32], wb[:, C_KR:C_KR + 32].rearrange("(k q) n -> q k n", q=128), (wr_,), (wtr,))
        ld(wv[:, :, 32:48], wb[:, C_KR + 16:C_KR + 32].rearrange("(k q) n -> q k n", q=128), (wr_, wtr), (wtr,))
        ld(wv[:, :, 48:64], wb[:, C_KR:C_KR + 16].rearrange("(k q) n -> q k n", q=128), (wr_, wtr), (wtr,))
        for j, (dst, dr_) in enumerate(((kr1, kr1_r), (kr2, kr2_r))):
            pt, pr = PS.get()
            for k in range(8):
                mm(pt[64:96, 0:N], wv[:, k, 32 * j:32 * j + 32], rhs[k], k == 0, k == 7, (wtr, xnT_r[k]), (pr,))
            cp("act", dst[64:96, 0:N], pt[64:96, 0:N], (pr,), (dr_,))
        tt("dve", kr1[64:96, 0:N], kr1[64:96, 0:N], rc[64:96, 0:N], ALU.mult, (kr1_r, rc_r), (kr1_r,))
        tt("dve", kr2[64:96, 0:N], kr2[64:96, 0:N], rs_[64:96, 0:N], ALU.mult, (kr2_r, rs_r), (kr2_r,))
        tt("dve", kpeT[64:96, 0:N], kr1[64:96, 0:N], kr2[64:96, 0:N], ALU.add, (kr1_r, kr2_r), (kpeT_r,))
        if not lite:
            norm_fm([qlT[:, c, 0:N] for c in range(3)], qlT_r, 128, 384, gQ, gQ_r,
                    [qcT[:, c, 0:N] for c in range(3)], qcT_r, N)
        norm_fm([kvT[:, c, 0:N] for c in range(2)], kvT_r, 128, 256, gKV, gKV_r,
                [ckvT[:, c, 0:N] for c in range(2)], ckvT_r, N,
                out2=([ckvTb[:, c, 0:N] for c in range(2)], ckvTb_r))

    def head_norm(src, src_r, gain, gain_r, dst, dst_r, N, extra_scale=None):
        sq, sqr = SQ.get()
        act(sq[0:96, 0:N], src, AF.Square, (src_r,), (sqr,))
        pt, pr = PS.get()
        mm(pt[0:96, 0:N], ones[0:96, 0:96], sq[0:96, 0:N], True, True, (sqr, ones_r), (pr,))
        act(lnt[0:96, 0:N], pt[0:96, 0:N], AF.Ln, (pr, eps_r), (lnt_r,), scale=1.0 / 96, bias=epsT[0:96, :])
        act(rstd[0:96, 0:N], lnt[0:96, 0:N], AF.Exp, (lnt_r,), (rstd_r,), scale=-0.5)
        stt(dst, src, gain[0:96, 0:1], rstd[0:96, 0:N], ALU.mult, ALU.mult, (src_r, gain_r, rstd_r), (dst_r,))

    def q_heads(N):
        rhs = [qcT[:, c, 0:N] for c in range(3)]
        wb, wr_ = W["w_uq"][1], W["w_uq"][2]
        for h in range(8):
            tick()
            wt, wtr = WR.get()
            wv = wt[:, 0:3 * 128].rearrange("q (k n) -> q k n", k=3)
            b0 = 96 * h
            ld(wv[:, :, 0:96], wb[:, b0:b0 + 96].rearrange("(k q) n -> q k n", q=128), (wr_,), (wtr,))
            ld(wv[:, :, 96:112], wb[:, b0 + 80:b0 + 96].rearrange("(k q) n -> q k n", q=128), (wr_, wtr), (wtr,))
            ld(wv[:, :, 112:128], wb[:, b0 + 64:b0 + 80].rearrange("(k q) n -> q k n", q=128), (wr_, wtr), (wtr,))
            pt, pr = PS.get()
            for k in range(3):
                mm(pt[0:96, 0:N], wv[:, k, 0:96], rhs[k], k == 0, k == 2, (wtr, qcT_r[k]), (pr,))
            pt2, pr2 = PS.get()
            for k in range(3):
                mm(pt2[64:96, 0:N], wv[:, k, 96:128], rhs[k], k == 0, k == 2, (wtr, qcT_r[k]), (pr2,))
            qh, qhr = QH.get()
            cp("act", qh[0:64, 0:N], pt[0:64, 0:N], (pr,), (qhr,))
            tt("dve", qs1[64:96, 0:N], pt[64:96, 0:N], rc[64:96, 0:N], ALU.mult, (pr, rc_r), (qs1_r,))
            tt("dve", qs2[64:96, 0:N], pt2[64:96, 0:N], rs_[64:96, 0:N], ALU.mult, (pr2, rs_r), (qs2_r,))
            tt("dve", qh[64:96, 0:N], qs1[64:96, 0:N], qs2[64:96, 0:N], ALU.add, (qs1_r, qs2_r, qhr),
               (qhr,))
            head_norm(qh[:, 0:N], qhr, gqh, gqh_r, qhb[:, h, 0:N], qhb_r[h], N)

    def back(ydst, t0, N):
        nsub = N // 128
        crow = [96] * 5 + [32]

        def epi_glu(m, pt, pr, mr):
            ta, tar = tmpA.get()
            act(ta[0:mr, 0:N], pt[0:mr, 0:N], AF.Sigmoid, (pr,), (tar,))
            tt("dve", aoT[0:mr, m, 0:N], ta[0:mr, 0:N], zT[0:mr, m, 0:N], ALU.mult, (tar, zT_r[m]), (aoT_r[m],))
        linear_fm("w_glu", 0, 512, [zTb[0:crow[c], c, 0:N] for c in range(6)], zTb_r, N, epi_glu, krows=96, mchunk=96)

        def epi_a(m, pt, pr, mr):
            tt("dve", mgTb[:, m, 0:N], pt[:, 0:N], sga[:, m, 0:N], ALU.mult, (pr, sga_r[m]), (mgTb_r[m],))
        linear_fm("w_branch_a", 0, D, [aoT[0:crow[c], c, 0:N] for c in range(6)], aoT_r, N, epi_a, krows=96)

        def epi_b(m, pt, pr, mr):
            ta, tar = tmpA.get()
            tt("dve", ta[:, 0:N], pt[:, 0:N], sgb[:, m, 0:N], ALU.mult, (pr, sgb_r[m]), (tar,))
            tt("pool", mgTb[:, m, 0:N], ta[:, 0:N], mgTb[:, m, 0:N], ALU.add, (tar, mgTb_r[m]), (mgTb_r[m],))
        linear_fm("w_branch_b", 0, D, [boT[:, c, 0:N] for c in range(4)], boT_r, N, epi_b)

        def epi_o(m, pt, pr, mr):
            tt("dve", xT[:, m, 0:N], pt[:, 0:N], xT[:, m, 0:N], ALU.add, (pr, xT_r[m]), (xT_r[m],))
        linear_fm("w_out", 0, D, [mgTb[:, c, 0:N] for c in range(8)], mgTb_r, N, epi_o)
        norm_fm([xT[:, c, 0:N] for c in range(8)], xT_r, 128, D, gF, gF_r,
                [xnT[:, c, 0:N] for c in range(8)], xnT_r, N)
        rhs = [xnT[:, c, 0:N] for c in range(8)]

        def epi_g(m, pt, pr, mr):
            ta, tar = tmpA.get()
            act(ta[:, 0:N], pt[:, 0:N], AF.Sigmoid, (pr,), (tar,))
            tt("dve", hT[:, m, 0:N], pt[:, 0:N], ta[:, 0:N], ALU.mult, (pr, tar), (hT_r[m],))
        linear_fm("w_gate", 0, D_FF, rhs, xnT_r, N, epi_g)

        def epi_up(m, pt, pr, mr):
            tt("dve", hT[:, m, 0:N], pt[:, 0:N], hT[:, m, 0:N], ALU.mult, (pr, hT_r[m]), (hT_r[m],))
        linear_fm("w_up", 0, D_FF, rhs, xnT_r, N, epi_up)
        linear_fm("w_down", 0, D, [hT[:, c, 0:N] for c in range(22)], hT_r, N, epi_o)
        norm_fm([xT[:, c, 0:N] for c in range(8)], xT_r, 128, D, gP, gP_r,
                [xnT[:, c, 0:N] for c in range(8)], xnT_r, N)

        def epi_pg(m, pt, pr, mr):
            act(gtT[:, m, 0:N], pt[:, 0:N], AF.Sigmoid, (pr,), (gtT_r[m],))
        linear_fm("w_ple_gate", 0, D, rhs, xnT_r, N, epi_pg)

        def epi_pl(m, pt, pr, mr):
            ta, tar = tmpA.get()
            tt("dve", ta[:, 0:N], pt[:, 0:N], gtT[:, m, 0:N], ALU.mult, (pr, gtT_r[m]), (tar,))
            tt("pool", xT[:, m, 0:N], ta[:, 0:N], xT[:, m, 0:N], ALU.add, (tar, xT_r[m]), (xT_r[m],))
        linear_fm("w_ple", 0, D, [pT[:, c, 0:N] for c in range(2)], pT_r, N, epi_pl)
        for s in range(nsub):
            for c in range(8):
                pt, pr = PS.get()
                tr(pt[:, 0:128], xT[:, c, 128 * s:128 * s + 128], ident[:], (xT_r[c], ident_r), (pr,))
                cp("act" if c % 2 else "dve", xtok[:, s, 128 * c:128 * c + 128], pt[:, 0:128], (pr,), (xtok_r[s],))
            ld(ydst[t0 + 128 * s:t0 + 128 * s + 128, :], xtok[:, s, :], (xtok_r[s],), ())

    def store_ckv_kpe(cdst, kdst, t0, N):
        nsub = N // 128
        for s in range(nsub):
            for c in range(2):
                pt, pr = PS.get()
                tr(pt[:, 0:128], ckvT[:, c, 128 * s:128 * s + 128], ident[:], (ckvT_r[c], ident_r), (pr,))
                cp("act", otok[:, s, 128 * c:128 * c + 128], pt[:, 0:128], (pr,), (otok_r[s],))
            ld(cdst[t0 + 128 * s:t0 + 128 * s + 128, :], otok[:, s, :], (otok_r[s],), ())
            pt, pr = PS.get()
            tr(pt[:, 0:32], kpeT[64:96, 128 * s:128 * s + 128], ident[64:96, 64:96], (kpeT_r, ident_r), (pr,))
            cp("act", ktok[:, s, :], pt[:, 0:32], (pr,), (ktok_r[s],))
            ld(kdst[t0 + 128 * s:t0 + 128 * s + 128, :], ktok[:, s, :], (ktok_r[s],), ())

    arena.reset()
    KB = Ring(p, "kb", 2, [96, KSEQ], BF16, arena=arena)
    VB = Ring(p, "vb", 2, [128, KSEQ // 128, 66], BF16, arena=arena)
    PB = Ring(p, "pb", 3, [128, NT], BF16, arena=arena)
    botok = arena.get([128, 2, 512], BF16); botok_r = [R(), R()]
    rec = arena.get([128, 4], F32); rec_r = R()

    def prompt_kv(t, N):
        rhs = [ckvTb[:, c, 0:N] for c in range(2)]
        wb, wr_ = W["w_uk"][1], W["w_uk"][2]
        wt, wtr = WR.get()
        wv = wt[:, 0:2 * 512].rearrange("q (k n) -> q k n", k=2)
        ld(wv, wb.rearrange("(k q) n -> q k n", q=128), (wr_,), (wtr,))
        for h in range(8):
            tick()
            pt, pr = PS.get()
            for k in range(2):
                mm(pt[0:64, 0:N], wv[:, k, 64 * h:64 * h + 64], rhs[k], k == 0, k == 1, (wtr, ckvTb_r[k]), (pr,))
            cp("act", kh[0:64, 0:N], pt[0:64, 0:N], (pr,), (kh_r,))
            cp("pool", kh[64:96, 0:N], kpeT[64:96, 0:N], (kpeT_r, kh_r), (kh_r,))
            khb_, khb_r_ = khb, khb_r
            head_norm(kh[:, 0:N], kh_r, gkh, gkh_r, khb_[:, 0:N], khb_r_, N)
            ld(Kscr[h, :, t * NT:t * NT + N], khb_[:, 0:N], (khb_r_,), (Kscr_r,))
        wb, wr_ = W["w_uv"][1], W["w_uv"][2]
        wt, wtr = WR.get()
        wv = wt[:, 0:2 * 512].rearrange("q (k n) -> q k n", k=2)
        ld(wv, wb.rearrange("(k q) n -> q k n", q=128), (wr_,), (wtr,))
        for s in range(N // 128):
            pt, pr = PS.get()
            for k in range(2):
                mm(pt[:, 0:512], ckvTb[:, k, 128 * s:128 * s + 128], wv[:, k, :], k == 0, k == 1,
                   (wtr, ckvTb_r[k]), (pr,))
            cp("act", vtok[:, s, :, 0:64], pt[:, 0:512].rearrange("q (h v) -> q h v", h=8), (pr,), (vtok_r[s],))
            ld(Vscr[(t * NT) // 128 + s], vtok[:, s, :, :].rearrange("q h v -> q (h v)"), (vtok_r[s],), (Vscr_r,))

    def prompt_attn(t, N):
        t = t + NPRE
        nkb = (t * NT + N) // 128
        nsub = N // 128
        for h in range(8):
            tick()
            kb, kbr = KB.get()
            ld(kb[:, 0:nkb * 128], Kscr[h, :, 0:nkb * 128], (Kscr_r,), (kbr,))
            vb, vbr = VB.get()
            ld(vb[:, 0:nkb, 0:65], Vscr[0:nkb, :, 65 * h:65 * h + 65].rearrange("b q v -> q b v"), (Vscr_r,), (vbr,))
            op_, opr = PSO.get()
            accs = [(op_, opr)] if nsub == 1 else [(op_, opr), PSO.get()]
            def pv(j, pb, pbr, q0):
                for s in range(nsub):
                    if 128 * s < q0:
                        continue
                    jl = (t * NT) // 128 + s
                    mm(accs[s][0][:, 0:65], pb[:, 128 * s:128 * s + 128], vb[:, j, 0:65], j == 0, j == jl,
                       (pbr, vbr), (accs[s][1],))
            pend = None
            for j in range(nkb):
                q0 = max(0, j - (t * NT) // 128) * 128
                st_, str_ = PS.get()
                mm(st_[:, q0:N], kb[:, 128 * j:128 * j + 128], qhb[:, h, q0:N], True, True, (kbr, qhb_r[h]), (str_,))
                pb, pbr = PB.get()
                if j < 2 * NPRE:
                    act(pb[:, q0:N], st_[:, q0:N], AF.Exp, (str_, pmask_r), (pbr,), scale=ATTN_SCALE, bias=pmask[:, 0:1])
                else:
                    act(pb[:, q0:N], st_[:, q0:N], AF.Exp, (str_,), (pbr,), scale=ATTN_SCALE)
                if 128 * j >= t * NT:
                    tt("pool", pb[:, q0:q0 + 128], pb[:, q0:q0 + 128], tri[:], ALU.mult, (pbr, tri_r), (pbr,))
                if pend is not None:
                    pv(*pend)
                pend = (j, pb, pbr, q0)
            pv(*pend)
            for s in range(nsub):
                oa, oar = accs[s]
                p.op("dve", lambda e, oa=oa, s=s: e.reciprocal(out=rec[:, s:s + 1], in_=oa[:, 64:65]), (oar,), (rec_r,))
                ts("dve", botok[:, s, 64 * h:64 * h + 64], oa[:, 0:64], rec[:, s:s + 1], None, ALU.mult, None,
                   (oar, rec_r), (botok_r[s],))
        for s in range(nsub):
            for c in range(4):
                pt, pr = PS.get()
                ptb = pt[:].bitcast(BF16)
                tr(ptb[:, 0:128], botok[:, s, 128 * c:128 * c + 128], identb[:], (botok_r[s], identb_r), (pr,))
                cp("act", boT[:, c, 128 * s:128 * s + 128], ptb[:, 0:128], (pr,), (boT_r[c],))

    arena.reset()
    NG = NPG // 4
    idxc = arena.get([128, 16, NG], I32); idx_r = R()
    goff = arena.get([128, 2], I32)
    PG4 = Ring(p, "pg4", 4, [128, 1024], BF16, arena=arena)
    PK4 = Ring(p, "pk4", 4, [128, 128], BF16, arena=arena)
    PGT = Ring(p, "pgt", 3, [128, 256], BF16, arena=arena)
    PKT = Ring(p, "pkt", 3, [32, 128], BF16, arena=arena)
    KSQ = Ring(p, "ksq", 2, [128, 512], BF16, arena=arena)
    KS2 = Ring(p, "ks2", 2, [128, 128], BF16, arena=arena)
    KSS = Ring(p, "kss", 4, [128, 4], F32, arena=arena)
    SS = Ring(p, "ss", 6, [128, 12], F32, arena=arena)
    SC = Ring(p, "sc", 5, [128, 64], F32, arena=arena)
    PP = Ring(p, "pp", 4, [128, 64], BF16, arena=arena)
    Qab = arena.get([128, 2, 16, 64], BF16); Qab_r = R()
    Qrp = arena.get([32, 16, 64], BF16); Qrp_r = R()
    qg = arena.get([96, 128], BF16); qg_r = R()
    OnT = arena.get([128, 2, 16, 64], BF16); OnT_r = R()
    On = arena.get([64, 256], BF16); On_r = R()
    rec2 = arena.get([64, 2], F32); rec2_r = R()
    ctokb = arena.get([128, 256], BF16); ctokb_r = R()
    kpeTb = arena.get([32, 128], BF16); kpeTb_r = R()
    kpetok = arena.get([128, 32], BF16); kpetok_r = R()
    kssn = arena.get([128, 2], F32); kssn_r = R()
    wukS = arena.get([128, 2, 512], BF16); wukS_r = R()
    wukT = arena.get([64, 8, 256], BF16); wukT_r = R()
    wuvS = arena.get([128, 2, 512], BF16); wuvS_r = R()
    for nm in ("h0r", "h0i", "hpr", "hpi"):
        SMP[nm] = arena.get([128, NU, 16], F32)
    SMP["fsr"], SMP["fsi"] = SMP["h0r"], SMP["h0i"]
    h0r, h0i, hpr, hpi, fsr, fsi = (SMP[k] for k in ("h0r", "h0i", "hpr", "hpi", "fsr", "fsi"))
    sttok = xtok[0:16, :, :].rearrange("q s d -> q (s d)")
    sttok_rs = (xtok_r[0], xtok_r[1])

    def sample_setup():
        p.op("pool", lambda e: e.iota(goff[:, 0:1], [[0, 1]], base=0, channel_multiplier=1), (), (idx_r,))
        p.op("dve", lambda e: e.tensor_single_scalar(out=goff[:, 0:1], in_=goff[:, 0:1], scalar=31, op=ALU.bitwise_and),
             (idx_r,), (idx_r,))

    def sample_ptab(st):
        src = ptab[16 * st:16 * st + 16, :].rearrange("s (m a) -> a s m", a=4)
        for a in range(4):
            ld(idxc[32 * a:32 * a + 32, :, :], src[a].partition_broadcast(32), (idx_r,), (idx_r,), q="pool")
        ts("pool", idxc[:], idxc[:], 32, None, ALU.mult, None, (idx_r,), (idx_r,))
        tt("pool", idxc[:], idxc[:], goff[:, 0:1].rearrange("q (a b) -> q a b", b=1).broadcast_to([128, 16, NG]),
           ALU.add, (idx_r,), (idx_r,))

    def sample_attn(st):
        N = 128
        for h in range(8):
            ts("dve", qg[0:96, :], qhb[:, h, 0:N], gkh[:, 0:1], None, ALU.mult, None, (qhb_r[h], gkh_r, qg_r), (qg_r,))
            for k in range(2):
                pt, pr = PS.get()
                mm(pt[:, 0:N], wukT[:, h, 128 * k:128 * k + 128], qg[0:64, :], True, True, (wukT_r, qg_r), (pr,))
                cp("act", Qab[:, k, :, 8 * h:8 * h + 8], pt[:, 0:N].rearrange("q (s t) -> q s t", t=8), (pr, Qab_r),
                   (Qab_r,))
            pt, pr = PS.get()
            mm(pt[0:32, 0:N], identb[64:96, 64:96], qg[64:96, :], True, True, (identb_r, qg_r), (pr,))
            cp("act", Qrp[:, :, 8 * h:8 * h + 8], pt[0:32, 0:N].rearrange("q (s t) -> q s t", t=8), (pr, Qrp_r),
               (Qrp_r,))
        for c in range(2):
            pt, pr = PS.get()
            ptb = pt[:].bitcast(BF16)
            tr(ptb[:, 0:128], ckvTb[:, c, 0:N], identb[:], (ckvTb_r[c], identb_r), (pr,))
            cp("act", ctokb[:, 128 * c:128 * c + 128], ptb[:, 0:128], (pr, ctokb_r), (ctokb_r,))
        pt, pr = PS.get()
        mm(pt[0:32, 0:N], ident[64:96, 64:96], kpeT[64:96, 0:N], True, True, (ident_r, kpeT_r), (pr,))
        cp("act", kpeTb[:, :], pt[0:32, 0:N], (pr,), (kpeTb_r,))
        pt, pr = PS.get()
        tr(pt[:, 0:32], kpeT[64:96, 0:N], ident[64:96, 64:96], (kpeT_r, ident_r), (pr,))
        cp("act", kpetok[:, :], pt[:, 0:32], (pr,), (kpetok_r,))

        ks2, ks2r = KS2.get()
        act(ks2[:, 0:32], kpetok[:, :], AF.Square, (kpetok_r,), (ks2r,))
        p.op("dve", lambda e, ks2=ks2: e.tensor_reduce(out=kssn[:, 0:1], in_=ks2[:, 0:32], axis=AX.X, op=ALU.add),
             (ks2r,), (kssn_r,))
        gathers = [(b, m) for b in range(16) for m in range(NG)]
        gbuf = {}

        def issue_gather(gi):
            if gi >= len(gathers) or gi in gbuf:
                return
            b, m = gathers[gi]
            pg, pgr = PG4.get()
            pk, pkr = PK4.get()
            p.dma("pool", lambda e, pg=pg, b=b, m=m: e.indirect_dma_start(
                out=pg[:, :], out_offset=None, in_=cache_ckv,
                in_offset=bass.IndirectOffsetOnAxis(ap=idxc[:, b, m:m + 1].bitcast(U32), axis=0)),
                (idx_r,), (pgr,))
            p.dma("pool", lambda e, pk=pk, b=b, m=m: e.indirect_dma_start(
                out=pk[:, :], out_offset=None, in_=cache_kpe,
                in_offset=bass.IndirectOffsetOnAxis(ap=idxc[:, b, m:m + 1].bitcast(U32), axis=0)),
                (idx_r,), (pkr,))
            ks2, ks2r = KS2.get()
            tt("pool", ks2[:, :], pk[:, :], pk[:, :], ALU.mult, (pkr,), (ks2r,))
            kss, kssr = KSS.get()
            p.op("dve", lambda e, kss=kss, ks2=ks2: e.tensor_reduce(
                out=kss[:, 0:4], in_=ks2[:, :].rearrange("q (c d) -> q c d", c=4), axis=AX.X, op=ALU.add), (ks2r,), (kssr,))
            gbuf[gi] = (pg, pgr, pk, pkr, kss, kssr)

        blocks = []
        for b in range(16):
            for m in range(NG):
                for c in range(4):
                    blocks.append(dict(b=b, gi=b * NG + m, c=c, new=False, first=(m == 0 and c == 0), last=False))
            blocks.append(dict(b=b, gi=None, c=0, new=True, first=(NG == 0), last=True))
        accs = {}

        def stage_a(bl):
            b = bl["b"]
            if bl["first"]:
                accs[b] = PSO.get()
            if bl["new"]:
                bl["tok"], bl["tok_r"] = ctokb[:, :], (ctokb_r,)
                bl["T"], bl["T_r"] = [ckvTb[:, 0, 0:N], ckvTb[:, 1, 0:N]], tuple(ckvTb_r)
                bl["kT"], bl["kT_r"] = kpeTb[:, :], (kpeTb_r,)
                bl["ktok"], bl["ktok_r"] = kpetok[:, :], (kpetok_r,)
                bl["kss"], bl["kss_r"] = kssn[:, 0:1], kssn_r
            else:
                gi, c = bl["gi"], bl["c"]
                if c == 0:
                    issue_gather(gi); issue_gather(gi + 1); issue_gather(gi + 2)
                pg, pgr, pk, pkr, kss, kssr = gbuf[gi]
                bl["kss"], bl["kss_r"] = kss[:, c:c + 1], kssr
                bl["tok"], bl["tok_r"] = pg[:, 256 * c:256 * c + 256], (pgr,)
                bl["ktok"], bl["ktok_r"] = pk[:, 32 * c:32 * c + 32], (pkr,)
                import os
                DBG2 = os.environ.get("KDBG", "")
                if "notr" in DBG2:
                    bl["T"], bl["T_r"] = [ckvTb[:, 0, 0:N], ckvTb[:, 1, 0:N]], tuple(ckvTb_r)
                    bl["kT"], bl["kT_r"] = kpeTb[:, :], (kpeTb_r,)
                else:
                    pt, pr = PS.get()
                    ptb = pt[:].bitcast(BF16)
                    for k in range(2):
                        tr(ptb[:, 128 * k:128 * k + 128], pg[:, 256 * c + 128 * k:256 * c + 128 * k + 128], identb[:],
                           (pgr, identb_r), (pr,))
                    pgT, pgTr = PGT.get()
                    cp("dve", pgT[:, :], ptb[:, 0:256], (pr,), (pgTr,))
                    pt2, pr2 = PS.get()
                    ptb2 = pt2[:].bitcast(BF16)
                    tr(ptb2[0:32, 0:128], pk[:, 32 * c:32 * c + 32], identb[:], (pkr, identb_r), (pr2,))
                    pkT, pkTr = PKT.get()
                    cp("dve", pkT[:, :], ptb2[0:32, 0:128], (pr2,), (pkTr,))
                    bl["T"], bl["T_r"] = [pgT[:, 0:128], pgT[:, 128:256]], (pgTr,)
                    bl["kT"], bl["kT_r"] = pkT[:, :], (pkTr,)
            kp, kpr = PS.get()
            for k in range(2):
                mm(kp[:, 0:512], bl["T"][k], wukS[:, k, :], k == 0, k == 1, bl["T_r"] + (wukS_r,), (kpr,))
            sp_, spr = SPB.get()
            for k in range(2):
                mm(sp_[:, 0:64], bl["T"][k], Qab[:, k, b, :], k == 0, False, bl["T_r"] + (Qab_r,), (spr,))
            mm(sp_[:, 0:64], bl["kT"], Qrp[:, b, :], False, True, bl["kT_r"] + (Qrp_r,), (spr,))
            bl["sp"], bl["sp_r"] = sp_, spr
            ksq, ksqr = KSQ.get()
            act(ksq[:], kp[:, 0:512], AF.Square, (kpr,), (ksqr,))
            ss, ssr = SS.get()
            p.op("dve", lambda e, ss=ss, ksq=ksq: e.tensor_reduce(
                out=ss[:, 0:8], in_=ksq[:].rearrange("q (h d) -> q h d", h=8), axis=AX.X, op=ALU.add), (ksqr,), (ssr,))
            ts("dve", ss[:, 0:8], ss[:, 0:8], bl["kss"], 1.0 / 96, ALU.add, ALU.mult, (ssr, bl["kss_r"]), (ssr,))
            bl["ss"], bl["ss_r"] = ss, ssr

        def stage_b(bl):
            ss, ssr = bl["ss"], bl["ss_r"]
            act(ss[:, 0:8], ss[:, 0:8], AF.Ln, (ssr, eps_r), (ssr,), bias=epsT[:, :])
            act(ss[:, 0:8], ss[:, 0:8], AF.Exp, (ssr,), (ssr,), scale=-0.5)
            sc, scr = SC.get()
            tt("dve", sc[:].rearrange("q (h t) -> q h t", h=8), bl["sp"][:, 0:64].rearrange("q (h t) -> q h t", h=8),
               ss[:, 0:8].rearrange("q (h o) -> q h o", o=1).broadcast_to([128, 8, 8]), ALU.mult, (bl["sp_r"], ssr), (scr,))
            bl["sc"], bl["sc_r"] = sc, scr

        def stage_c(bl):
            b = bl["b"]
            sc, scr = bl["sc"], bl["sc_r"]
            pp, ppr = PP.get()
            if not bl["new"]:
                act(pp[:], sc[:], AF.Exp, (scr,), (ppr,), scale=ATTN_SCALE)
            else:
                act(sc[:], sc[:], AF.Exp, (scr,), (scr,), scale=ATTN_SCALE)
                tt("dve", pp[:], sc[:], masks[:, 64 * b:64 * b + 64], ALU.mult, (scr, masks_r), (ppr,))
            acc, acc_r = accs[b]
            mm(acc[0:64, 0:256], pp[:], bl["tok"], bl["first"], False, (ppr,) + bl["tok_r"], (acc_r,))
            mm(acc[0:64, 256:258], pp[:], ones[:, 0:2], False, bl["last"], (ppr, ones_r), (acc_r,))
            if bl["last"]:
                p.op("dve", lambda e, acc=acc: e.reciprocal(out=rec2[:, 0:1], in_=acc[0:64, 256:257]), (acc_r,), (rec2_r,))
                ts("dve", On[:, :], acc[0:64, 0:256], rec2[:, 0:1], None, ALU.mult, None, (acc_r, rec2_r), (On_r,))
                for c in range(2):
                    pt, pr = PS.get()
                    ptb = pt[:].bitcast(BF16)
                    tr(ptb[:, 0:64], On[:, 128 * c:128 * c + 128], identb[0:64, 0:64], (On_r, identb_r), (pr,))
                    cp("act", OnT[:, c, b, :], ptb[:, 0:64], (pr, OnT_r), (OnT_r,))

        nb = len(blocks)
        import os
        DBG = os.environ.get("KDBG", "")
        if "noblocks" in DBG:
            nb = 0
            blocks = []
        if "onlynew" in DBG:
            blocks = [dict(bl, first=True) for bl in blocks if bl["new"]]
            nb = len(blocks)
        L1, L2 = (0, 0) if "nolag" in DBG else (1, 2)
        for i in range(nb + 4):
            if i < nb:
                stage_a(blocks[i])
            if 0 <= i - L1 < nb:
                stage_b(blocks[i - L1])
            if 0 <= i - L2 < nb:
                stage_c(blocks[i - L2])
        for h in range(8):
            pt, pr = PS.get()
            po = 64 * (h % 2)
            for k in range(2):
                mm(pt[po:po + 64, 0:N], wuvS[:, k, 64 * h:64 * h + 64],
                   OnT[:, k, :, 8 * h:8 * h + 8], k == 0, k == 1, (wuvS_r, OnT_r), (pr,))
            cp("act", boT[po:po + 64, h // 2, 0:N], pt[po:po + 64, 0:N], (pr, boT_r[h // 2]), (boT_r[h // 2],))

    for t in range(NPRE):
        front(x_pre, None, t * NT, NT, ropec_pre[:, t * NT:(t + 1) * NT], ropes_pre[:, t * NT:(t + 1) * NT], lite=True)
        IL[0], IL[1] = ssm_gen(NT, False, lite=True), 2.0
        prompt_kv(t, NT)
        drain()
    for t in range(NPT):
        front(x_p, p_p, t * NT, NT, ropec_p[:, t * NT:(t + 1) * NT], ropes_p[:, t * NT:(t + 1) * NT])
        store_ckv_kpe(ckv_p, kpe_p, t * NT, NT)
        IL[0], IL[1] = ssm_gen(NT, False), 0.5
        q_heads(NT)
        prompt_kv(NPRE + t, NT)
        IL[1] = 1.0
        prompt_attn(t, NT)
        drain()
        back(y_p, t * NT, NT)
    fo = sb("fo", [128, NU, 2]); fo_r = R()
    t1_, t2_ = sm["t1"], sm["t2"]
    tt("dve", t1_[:], carry[:, :, 0], sm["fre"][:], ALU.mult, tuple(carry_rs) + (ssm_r,), S)
    tt("dve", t2_[:], carry[:, :, 1], sm["fim"][:], ALU.mult, tuple(carry_rs) + (ssm_r,), S)
    tt("dve", fo[:, :, 0], t1_[:], t2_[:], ALU.subtract, S, (fo_r,))
    tt("dve", t1_[:], carry[:, :, 0], sm["fim"][:], ALU.mult, tuple(carry_rs) + (ssm_r,), S)
    tt("dve", t2_[:], carry[:, :, 1], sm["fre"][:], ALU.mult, tuple(carry_rs) + (ssm_r,), S)
    tt("dve", fo[:, :, 1], t1_[:], t2_[:], ALU.add, S, (fo_r,))
    with nc.allow_non_contiguous_dma(reason="final state store"):
        for ri, dst in enumerate((sre_p, sim_p)):
            ld(dst.rearrange("o (u g n) -> o g n u", u=NU, g=2)[0].rearrange("g n u -> (g n) u"), fo[:, :, ri],
               (fo_r,), ())

    p.barrier()
    if NST > 0:
        sample_setup()
        ld(wukS[:], W["w_uk"][1].rearrange("(k q) n -> q k n", q=128), (W["w_uk"][2],), (wukS_r,))
        ld(wuvS[:], W["w_uv"][1].rearrange("(k q) n -> q k n", q=128), (W["w_uv"][2],), (wuvS_r,))
        ld(wukT[:], W["w_ukT"][1].rearrange("(h d) c -> d h c", h=8), (W["w_ukT"][2],), (wukT_r,))
    for st in range(NST):
        for ri, (src, dst) in enumerate(((st_re, h0r), (st_im, h0i))):
            ld(sttok[:, :], src[16 * st:16 * st + 16, :], (), sttok_rs)
            for u in range(NU):
                pt, pr = PS.get()
                tr(pt[:, 0:16], sttok[:, 128 * u:128 * u + 128], ident[0:16, 0:16], sttok_rs + (ident_r,), (pr,))
                cp("act", dst[:, u, :], pt[:, 0:16], (pr, h0_r), (h0_r,))
        b16 = lambda ap: ap.rearrange("q (u o) -> q u o", o=1).broadcast_to([128, NU, 16])
        H0 = (h0_r, ssm_r)
        tt("dve", ct[0][:], h0r[:], b16(sm["fir"][:]), ALU.mult, H0, (cw_r,))
        tt("dve", ct[1][:], h0i[:], b16(sm["fii"][:]), ALU.mult, H0, (cw_r,))
        tt("dve", hpr[:], ct[0][:], ct[1][:], ALU.subtract, (cw_r,), (h0_r,))
        tt("dve", ct[0][:], h0r[:], b16(sm["fii"][:]), ALU.mult, H0, (cw_r,))
        tt("dve", ct[1][:], h0i[:], b16(sm["fir"][:]), ALU.mult, H0, (cw_r,))
        tt("dve", hpi[:], ct[0][:], ct[1][:], ALU.add, (cw_r,), (h0_r,))
        sample_ptab(st)
        build_decs()
        front(x_s, p_s, st * 128, 128, ropec_s, ropes_s)
        store_ckv_kpe(ckv_s, kpe_s, st * 128, 128)
        ssm_tile(128, True)
        FS = tuple(fs_rs) + (ssm_r,)
        tt("dve", ct[0][:], fsr[:], b16(sm["fre"][:]), ALU.mult, FS, (cw_r,))
        tt("dve", ct[1][:], fsi[:], b16(sm["fim"][:]), ALU.mult, FS, (cw_r,))
        tt("dve", hpr[:], ct[0][:], ct[1][:], ALU.subtract, (cw_r,), (h0_r,))
        tt("dve", ct[0][:], fsr[:], b16(sm["fim"][:]), ALU.mult, FS, (cw_r,))
        tt("dve", ct[1][:], fsi[:], b16(sm["fre"][:]), ALU.mult, FS, (cw_r,))
        tt("dve", hpi[:], ct[0][:], ct[1][:], ALU.add, (cw_r,), (h0_r,))
        for ri, (src, dst) in enumerate(((hpr, sre_s), (hpi, sim_s))):
            for u in range(NU):
                pt, pr = PS.get()
                tr(pt[0:16, 0:128], src[:, u, :], ident[:], (h0_r, ident_r), (pr,))
                cp("act", sttok[:, 128 * u:128 * u + 128], pt[0:16, 0:128], (pr,) + sttok_rs, sttok_rs)
            ld(dst[16 * st:16 * st + 16, :], sttok[:, :], sttok_rs, ())
        q_heads(128)
        sample_attn(st)
        back(y_s, st * 128, 128)

    p.finish()
    return nc, es


def _prep(cfg, inp, core, ncores):
    SEQ, NS, NPOOL, NPG = cfg["SEQ"], cfg["NS"], cfg["NPOOL"], cfg["NPG"]
    NPRE = cfg.get("NPRE", 0)
    f = lambda a: np.ascontiguousarray(np.asarray(a, dtype=np.float32))
    m = {}
    if NPRE:
        sq, half = core // 2, core % 2
        pos0 = half * SEQ
        m["x_p"] = f(inp["x_prompt"][sq, pos0:pos0 + SEQ]); m["p_p"] = f(inp["p_prompt"][0, sq, pos0:pos0 + SEQ])
        m["x_pre"] = f(inp["x_prompt"][sq, 0:SEQ]) if half else np.zeros((SEQ, D), np.float32)
        m["pmask"] = np.full((128, 1), 0.0 if half else -30000.0, np.float32)
    else:
        pos0 = 0
        m["x_p"] = f(inp["x_prompt"][core]); m["p_p"] = f(inp["p_prompt"][0, core])
        m["pmask"] = np.zeros((128, 1), np.float32)
    sl = slice(core * NS, (core + 1) * NS)
    m["x_s"] = f(inp["x_sample"][sl]).reshape(NS * 8, D); m["p_s"] = f(inp["p_sample"][0, sl]).reshape(NS * 8, PLE)
    m["cache_ckv"] = f(inp["cache_ckv"][0]).reshape(NPOOL * 32, 1024)
    m["cache_kpe"] = f(inp["cache_kpe"][0]).reshape(NPOOL * 32, 128)
    m["st_re"] = f(inp["state_ssm_re"][0, sl]).reshape(NS, 2048)
    m["st_im"] = f(inp["state_ssm_im"][0, sl]).reshape(NS, 2048)
    m["ptab"] = np.ascontiguousarray(np.asarray(inp["page_table"][sl], dtype=np.int32))
    m["w_in"] = f(inp["w_in"][0]); m["w_uq"] = f(inp["w_uq"][0]).reshape(384, 768)
    m["w_uk"] = f(inp["w_uk"][0]).reshape(256, 512); m["w_uv"] = f(inp["w_uv"][0]).reshape(256, 512)
    m["w_ukT"] = f(np.transpose(np.asarray(inp["w_uk"][0]), (1, 2, 0))).reshape(512, 256)
    for nm in ("w_glu", "w_branch_a", "w_branch_b", "w_out", "w_gate", "w_up", "w_down", "w_ple_gate", "w_ple"):
        m[nm] = f(inp[nm][0])
    m["g_attn"] = f(inp["norm_attn_g"][0]); m["g_ffn"] = f(inp["norm_ffn_g"][0]); m["g_ple"] = f(inp["norm_ple_g"][0])
    m["g_q"] = f(inp["q_norm_g"][0]); m["g_kv"] = f(inp["kv_norm_g"][0])
    m["g_qn"] = f(inp["q_norm_nope_g"][0]); m["g_qr"] = f(inp["q_norm_rope_g"][0])
    m["g_kn"] = f(inp["k_norm_nope_g"][0]); m["g_kr"] = f(inp["k_norm_rope_g"][0])
    m["lam_re"] = f(inp["ssm_lam_re"][0]); m["lam_im"] = f(inp["ssm_lam_im"][0]); m["log_dt"] = f(inp["ssm_log_dt"][0])
    m["b_reT"] = f(np.transpose(np.asarray(inp["ssm_b_re"][0]), (0, 2, 1)))
    m["b_imT"] = f(np.transpose(np.asarray(inp["ssm_b_im"][0]), (0, 2, 1)))
    m["c_reT"] = f(np.transpose(np.asarray(inp["ssm_c_re"][0]), (0, 2, 1)))
    m["c_imT"] = f(np.transpose(np.asarray(inp["ssm_c_im"][0]), (0, 2, 1)))
    m["ssm_d"] = f(inp["ssm_d"][0])
    m["ident"] = np.eye(128, dtype=np.float32)
    half = 16
    inv = (10000.0 ** (-np.arange(half, dtype=np.float32) / half)).astype(np.float32)

    def tabs(pos):
        ang = pos.astype(np.float32)[None, :] * inv[:, None]
        c = np.cos(ang).astype(np.float32); s = np.sin(ang).astype(np.float32)
        return np.ascontiguousarray(np.concatenate([c, c], 0)), np.ascontiguousarray(np.concatenate([-s, s], 0))
    m["ropec_p"], m["ropes_p"] = tabs(pos0 + np.arange(SEQ))
    if NPRE:
        m["ropec_pre"], m["ropes_pre"] = tabs(np.arange(SEQ))
    cs, ss = tabs(cfg["PAST"] + np.arange(8))
    m["ropec_s"] = np.ascontiguousarray(np.tile(cs, (1, 16))); m["ropes_s"] = np.ascontiguousarray(np.tile(ss, (1, 16)))
    k = np.arange(128)
    m["tri"] = (k[None, :] >= k[:, None]).astype(np.float32)
    mk = np.zeros((128, 16, 8, 8), np.float32)
    for b in range(16):
        for tk in range(8):
            for tq in range(8):
                if tk <= tq:
                    mk[8 * b + tk, b, :, tq] = 1.0
    m["masks"] = mk.reshape(128, 16 * 64)
    return m


_CACHE = {}


def run(cfg, inp, ncores):
    key = tuple(sorted(cfg.items()))
    if key not in _CACHE:
        _CACHE[key] = build(cfg)
    nc, _ = _CACHE[key]
    in_maps = [_prep(cfg, inp, c, ncores) for c in range(ncores)]
    res = run_bass_kernel_spmd(nc, in_maps, core_ids=list(range(ncores)))
    return res.results


def kernel(**inp):
    B, SEQ = inp["x_prompt"].shape[0], inp["x_prompt"].shape[1]
    DB = inp["x_sample"].shape[0]
    NPOOL = inp["cache_ckv"].shape[1]
    NPG = inp["page_table"].shape[1]
    ncores = 2 * B
    NS = DB // ncores
    HS = SEQ // 2
    cfg = dict(SEQ=HS, NS=NS, NPOOL=NPOOL, NPG=NPG, PAST=NPG * 128, NPRE=HS // NT)
    rs = run(cfg, inp, ncores)
    cat = lambda k: np.stack([np.concatenate([rs[2 * b][k], rs[2 * b + 1][k]], 0) for b in range(B)], 0)
    y_p = cat("y_p")
    y_s = np.concatenate([r["y_s"].reshape(NS, 8, D) for r in rs], 0)
    ckv_p = cat("ckv_p")[None]; kpe_p = cat("kpe_p")[None]
    sre_p = np.stack([rs[2 * b + 1]["sre_p"] for b in range(B)], 0).reshape(1, B, 32, 64)
    sim_p = np.stack([rs[2 * b + 1]["sim_p"] for b in range(B)], 0).reshape(1, B, 32, 64)
    ckv_s = np.concatenate([r["ckv_s"].reshape(NS, 8, 256) for r in rs], 0)[None]
    kpe_s = np.concatenate([r["kpe_s"].reshape(NS, 8, 32) for r in rs], 0)[None]
    sre_s = np.concatenate([r["sre_s"].reshape(NS, 32, 64) for r in rs], 0)[None]
    sim_s = np.concatenate([r["sim_s"].reshape(NS, 32, 64) for r in rs], 0)[None]
    return tuple(np.ascontiguousarray(a, dtype=np.float32) for a in
                 (y_p, y_s, ckv_p, kpe_p, sre_p, sim_p, ckv_s, kpe_s, sre_s, sim_s))
```
